# Optimizing a Trainium2 kernel written in Bass

```python
import jax, jax.numpy as jnp
from jax import lax
import numpy as np

D_MODEL = 1024
BATCH = 8
SEQ = 2048
DEPTH = 1

CHUNK = 64
N_MEM = 256
HEAD_DIM = 64
D_MIX = D_MODEL
FOX_HEADS = D_MIX // 2 // HEAD_DIM
CHK_HEADS = D_MIX // 2 // HEAD_DIM
D_FOX = FOX_HEADS * HEAD_DIM
D_CHK = CHK_HEADS * HEAD_DIM
LEFT_CHUNKS = 8
BAND = (LEFT_CHUNKS + 1) * CHUNK
MAX_REL = 128
N_REL = 2 * MAX_REL + 1
Q_BLOCK = 128
MEM_HEADS = 4
MEM_HEAD_DIM = D_MODEL // MEM_HEADS
D_FF = 4 * D_MODEL
EPS = 1e-6
D_IN = 3 * D_FOX + FOX_HEADS + 3 * D_CHK

kernel_name = 'hybrid_fox_chunkrel_memxattn_block'


def rmsnorm(x, g):
    xf = x.astype(jnp.float32)
    y = xf * lax.rsqrt(jnp.mean(xf * xf, axis=-1, keepdims=True) + EPS) * g.astype(jnp.float32)
    return y.astype(x.dtype)


def forgetting_attention(q, k, v, f_logit):
    S = q.shape[1]
    Dh = q.shape[-1]
    scale = Dh ** -0.5
    logf = jax.nn.log_sigmoid(f_logit.astype(jnp.float32))
    c = jnp.cumsum(logf, axis=1).transpose(0, 2, 1)
    pos = jnp.arange(S)
    outs = []
    for i in range(S // Q_BLOCK):
        q0, q1 = i * Q_BLOCK, (i + 1) * Q_BLOCK
        qb = q[:, q0:q1]
        kb = k[:, :q1]
        vb = v[:, :q1]
        logits = jnp.einsum('bqhd,bkhd->bhqk', qb, kb).astype(jnp.float32) * scale
        logits = logits + c[:, :, q0:q1, None] - c[:, :, None, :q1]
        causal = pos[q0:q1, None] >= pos[None, :q1]
        logits = jnp.where(causal[None, None], logits, -jnp.inf)
        p = jax.nn.softmax(logits, axis=-1).astype(v.dtype)
        outs.append(jnp.einsum('bhqk,bkhd->bqhd', p, vb))
    return jnp.concatenate(outs, axis=1)


def _rel_index():
    i = np.arange(CHUNK)[:, None]
    m = np.arange(BAND)[None, :]
    rel = i + LEFT_CHUNKS * CHUNK - m
    return np.clip(rel, -MAX_REL, MAX_REL) + MAX_REL


def chunked_relpos_attention(q, k, v, rel_table):
    B, S, H, Dh = q.shape
    NC = S // CHUNK
    scale = Dh ** -0.5
    qc = q.reshape(B, NC, CHUNK, H, Dh)
    pad = ((0, 0), (LEFT_CHUNKS * CHUNK, 0), (0, 0), (0, 0))
    kp = jnp.pad(k, pad).reshape(B, NC + LEFT_CHUNKS, CHUNK, H, Dh)
    vp = jnp.pad(v, pad).reshape(B, NC + LEFT_CHUNKS, CHUNK, H, Dh)
    kband = jnp.concatenate([kp[:, j:j + NC] for j in range(LEFT_CHUNKS + 1)], axis=2)
    vband = jnp.concatenate([vp[:, j:j + NC] for j in range(LEFT_CHUNKS + 1)], axis=2)
    bias = rel_table.astype(jnp.float32)[:, _rel_index()]
    key_pos = jnp.arange(NC)[:, None] * CHUNK + jnp.arange(BAND)[None, :] - LEFT_CHUNKS * CHUNK
    valid = key_pos >= 0
    logits = jnp.einsum('bcqhd,bckhd->bhcqk', qc, kband).astype(jnp.float32) * scale
    logits = logits + bias[None, :, None]
    logits = jnp.where(valid[None, None, :, None, :], logits, -jnp.inf)
    p = jax.nn.softmax(logits, axis=-1).astype(v.dtype)
    out = jnp.einsum('bhcqk,bckhd->bcqhd', p, vband)
    return out.reshape(B, S, H, Dh)


def memory_cross_attention(h, mem_n, w_mq, w_mk, w_mv, w_mo):
    B, S, _ = h.shape
    M = mem_n.shape[1]
    q = (h @ w_mq).reshape(B, S, MEM_HEADS, MEM_HEAD_DIM)
    k = (mem_n @ w_mk).reshape(B, M, MEM_HEADS, MEM_HEAD_DIM)
    v = (mem_n @ w_mv).reshape(B, M, MEM_HEADS, MEM_HEAD_DIM)
    logits = jnp.einsum('bshd,bmhd->bhsm', q, k).astype(jnp.float32) * (MEM_HEAD_DIM ** -0.5)
    p = jax.nn.softmax(logits, axis=-1).astype(v.dtype)
    o = jnp.einsum('bhsm,bmhd->bshd', p, v).reshape(B, S, D_MODEL)
    return o @ w_mo


def setup_inputs(seed: int = 0) -> dict:
    key = jax.random.key(seed)
    ks = jax.random.split(key, 24)
    f32 = jnp.float32

    def w(k, shape, fan_in):
        return jax.random.normal(k, shape, f32) * fan_in ** -0.5

    def gain(k, n):
        return 1.0 + 0.05 * jax.random.normal(k, (DEPTH, n), f32)

    return {
        'x': jax.random.normal(ks[0], (BATCH, SEQ, D_MODEL), f32),
        'mem': jax.random.normal(ks[1], (BATCH, N_MEM, D_MODEL), f32),
        'w_in': w(ks[2], (DEPTH, D_MODEL, D_IN), D_MODEL),
        'b_fgt': 3.0 + 0.1 * jax.random.normal(ks[3], (DEPTH, FOX_HEADS), f32),
        'rel_bias': 0.2 * jax.random.normal(ks[4], (DEPTH, CHK_HEADS, N_REL), f32),
        'g_fox_out': gain(ks[5], D_FOX),
        'g_chk_out': gain(ks[6], D_CHK),
        'w_out': w(ks[7], (DEPTH, D_MIX, D_MODEL), D_MIX),
        'g_mix_pre': gain(ks[8], D_MODEL),
        'g_mix_post': gain(ks[9], D_MODEL),
        'g_mem_kv': gain(ks[10], D_MODEL),
        'w_mq': w(ks[11], (DEPTH, D_MODEL, D_MODEL), D_MODEL),
        'w_mk': w(ks[12], (DEPTH, D_MODEL, D_MODEL), D_MODEL),
        'w_mv': w(ks[13], (DEPTH, D_MODEL, D_MODEL), D_MODEL),
        'w_mo': w(ks[14], (DEPTH, D_MODEL, D_MODEL), D_MODEL),
        'g_mem_pre': gain(ks[15], D_MODEL),
        'g_mem_post': gain(ks[16], D_MODEL),
        'w_ff1': w(ks[17], (DEPTH, D_MODEL, D_FF), D_MODEL),
        'w_ff2': w(ks[18], (DEPTH, D_FF, D_MODEL), D_FF),
        'g_ff_pre': gain(ks[19], D_MODEL),
        'g_ff_post': gain(ks[20], D_MODEL),
    }


def reference(x, mem, w_in, b_fgt, rel_bias, g_fox_out, g_chk_out, w_out, g_mix_pre, g_mix_post,
              g_mem_kv, w_mq, w_mk, w_mv, w_mo, g_mem_pre, g_mem_post,
              w_ff1, w_ff2, g_ff_pre, g_ff_post):
    B, S, _ = x.shape
    for l in range(DEPTH):
        h = rmsnorm(x, g_mix_pre[l])
        proj = h @ w_in[l]
        o0 = 0
        fq = proj[..., o0:o0 + D_FOX]; o0 += D_FOX
        fk = proj[..., o0:o0 + D_FOX]; o0 += D_FOX
        fv = proj[..., o0:o0 + D_FOX]; o0 += D_FOX
        f_logit = proj[..., o0:o0 + FOX_HEADS] + b_fgt[l]; o0 += FOX_HEADS
        cq = proj[..., o0:o0 + D_CHK]; o0 += D_CHK
        ck = proj[..., o0:o0 + D_CHK]; o0 += D_CHK
        cv = proj[..., o0:o0 + D_CHK]
        shp_f = (B, S, FOX_HEADS, HEAD_DIM)
        shp_c = (B, S, CHK_HEADS, HEAD_DIM)
        y_fox = forgetting_attention(fq.reshape(shp_f), fk.reshape(shp_f), fv.reshape(shp_f), f_logit)
        y_chk = chunked_relpos_attention(cq.reshape(shp_c), ck.reshape(shp_c), cv.reshape(shp_c), rel_bias[l])
        y = jnp.concatenate([rmsnorm(y_fox.reshape(B, S, D_FOX), g_fox_out[l]),
                             rmsnorm(y_chk.reshape(B, S, D_CHK), g_chk_out[l])], axis=-1)
        x = x + rmsnorm(y @ w_out[l], g_mix_post[l])
        h = rmsnorm(x, g_mem_pre[l])
        mem_n = rmsnorm(mem, g_mem_kv[l])
        y = memory_cross_attention(h, mem_n, w_mq[l], w_mk[l], w_mv[l], w_mo[l])
        x = x + rmsnorm(y, g_mem_post[l])
        h = rmsnorm(x, g_ff_pre[l])
        y = jnp.square(jax.nn.relu(h @ w_ff1[l])) @ w_ff2[l]
        x = x + rmsnorm(y, g_ff_post[l])
    return x
```

```python
import numpy as np
from contextlib import ExitStack
import concourse.bass as bass
import concourse.mybir as mybir
from concourse.bass_utils import run_bass_kernel_spmd

F32 = mybir.dt.float32
BF16 = mybir.dt.bfloat16
AF = mybir.ActivationFunctionType
ALU = mybir.AluOpType

ENGS = ["pe", "act", "dve", "pool", "sp"]
NDMASEM = 12
EPS = 1e-6
S_TOK = 2048
D = 1024
NSB = 4
D_IN = 3080
NW = 3


class Buf:
    __slots__ = ("name", "w", "rs")

    def __init__(self, name):
        self.name = name
        self.w = None
        self.rs = []


class Op:
    __slots__ = ("eng", "fn", "waits", "tok", "dma")


class Sched:
    def __init__(self):
        self.ops = {e: [] for e in ENGS}
        self.cnt = {e: 0 for e in ENGS}
        self.waited = {e: {} for e in ENGS}
        self.dma_n = {e: 0 for e in ENGS}

    def add(self, eng, fn, reads=(), writes=(), dma=False):
        deps = {}

        def need(tok):
            if tok is None:
                return
            k, v = tok
            if k == ("e", "pe") and eng == "pe" and not dma:
                return
            if deps.get(k, 0) < v:
                deps[k] = v

        for b in reads:
            need(b.w)
        for b in writes:
            need(b.w)
            for t in b.rs:
                need(t)
        op = Op()
        op.eng = eng
        op.fn = fn
        op.dma = dma
        if dma:
            n = self.dma_n[eng]
            self.dma_n[eng] = n + 1
            slot = n % NDMASEM
            use = n // NDMASEM
            key = ("d", eng, slot)
            if use > 0:
                need((key, 16 * use))
            op.tok = (key, 16 * (use + 1))
        else:
            self.cnt[eng] += 1
            op.tok = (("e", eng), self.cnt[eng])
        w = self.waited[eng]
        waits = []
        for k, v in deps.items():
            if w.get(k, 0) >= v:
                continue
            w[k] = v
            waits.append((k, v))
        op.waits = waits
        self.ops[eng].append(op)
        for b in reads:
            b.rs.append(op.tok)
        for b in writes:
            b.w = op.tok
            b.rs = []
        return op.tok

    def emit(self, nc, final_wait_eng="sp"):
        with ExitStack() as es:
            sems = {}
            for e in ENGS:
                sems[("e", e)] = es.enter_context(nc.semaphore("s_" + e))
                for i in range(min(NDMASEM, self.dma_n[e])):
                    sems[("d", e, i)] = es.enter_context(nc.semaphore(f"d_{e}_{i}"))
            finals = []
            for e in ENGS:
                if self.cnt[e]:
                    finals.append((("e", e), self.cnt[e]))
                n = self.dma_n[e]
                for i in range(min(NDMASEM, n)):
                    uses = (n - 1 - i) // NDMASEM + 1
                    finals.append((("d", e, i), 16 * uses))
            block = es.enter_context(nc.Block())

            def run(e, h):
                for op in self.ops[e]:
                    for k, v in op.waits:
                        h.wait_ge(sems[k], v)
                    inst = op.fn(h)
                    k, v = op.tok
                    inst.then_inc(sems[k], 16 if op.dma else 1)
                if e == final_wait_eng:
                    for k, v in finals:
                        h.wait_ge(sems[k], v)

            @block.tensor
            def _(t):
                run("pe", t)

            @block.scalar
            def _(t):
                run("act", t)

            @block.vector
            def _(t):
                run("dve", t)

            @block.gpsimd
            def _(t):
                run("pool", t)

            @block.sync
            def _(t):
                run("sp", t)


class Rot:
    def __init__(self, items):
        self.items = items
        self.i = 0

    def next(self):
        it = self.items[self.i % len(self.items)]
        self.i += 1
        return it


def host_consts():
    c = np.zeros((128, 1024), np.float32)
    i = np.arange(128)
    tri = (i[:, None] <= i[None, :])
    c[:, 0:128] = tri
    c[:, 128:256] = (i[None, :] <= i[:, None])
    c[:, 256:384] = 1.0
    c[:, 384:512] = 1.0 - ((i[:, None] >= 64) & (i[None, :] < 64))
    c[:, 512:640] = np.eye(128)
    c[:, 640:768] = tri
    c[:, 768:896] = 1.0
    c[:, 896:1024] = 1.0 - ((i[:, None] < 64) & (i[None, :] >= 64))
    return c


def build(stage=99):
    nc = bass.Bass("TRN2", target_bir_lowering=False)

    def din(name, shape):
        return nc.dram_tensor(name, shape, F32, kind="ExternalInput").ap()

    x_d = din("x", [S_TOK, D])
    mem_d = din("mem", [256, D])
    w_in_d = din("w_in", [D, D_IN])
    b_fgt_d = din("b_fgt", [8])
    rel_d = din("rel_bias", [8 * 257])
    g_fox_d = din("g_fox_out", [512])
    g_chk_d = din("g_chk_out", [512])
    w_out_d = din("w_out", [D, D])
    g_mix_pre_d = din("g_mix_pre", [D])
    g_mix_post_d = din("g_mix_post", [D])
    g_mem_kv_d = din("g_mem_kv", [D])
    w_mq_d = din("w_mq", [D, D])
    w_mk_d = din("w_mk", [D, D])
    w_mv_d = din("w_mv", [D, D])
    w_mo_d = din("w_mo", [D, D])
    g_mem_pre_d = din("g_mem_pre", [D])
    g_mem_post_d = din("g_mem_post", [D])
    w_ff1_d = din("w_ff1", [D, 4 * D])
    w_ff2_d = din("w_ff2", [4 * D, D])
    g_ff_pre_d = din("g_ff_pre", [D])
    g_ff_post_d = din("g_ff_post", [D])
    consts_d = din("consts", [128, 1024])
    out_d = nc.dram_tensor("out", [S_TOK, D], F32, kind="ExternalOutput").ap()
    RBL = 8 * 257 + 128
    rb2_d = nc.dram_tensor("rb2", [128, RBL], F32, kind="Internal").ap()

    S = Sched()
    es = ExitStack()

    def sb(name, shape, dt):
        return es.enter_context(nc.sbuf_tensor(name, shape, dt))

    def ps(name, shape, dt):
        return es.enter_context(nc.psum_tensor(name, shape, dt))

    cf32 = sb("cf32", [128, 512], F32)
    cbf = sb("cbf", [128, 512], BF16)
    ident_bf = cbf[:, 0:128]
    tri_bf = cbf[:, 128:256]
    ones_bf = cbf[:, 256:384]
    m4_bf = cbf[:, 384:512]
    tri_f = cf32[:, 0:128]
    lt_f = cf32[:, 128:256]
    ones_f = cf32[:, 256:384]
    m0_f = cf32[:, 384:512]
    B_cf32 = Buf("cf32")
    B_cbf = Buf("cbf")
    gbc = {k: sb("gbc_" + k, [128, D], F32) for k in ["mix_post", "mem_post", "ff_post"]}
    B_gbc = {k: Buf("gbc_" + k) for k in gbc}
    gcol = sb("gcol", [128, 40], F32)
    B_gc = {k: Buf("gcol_" + k) for k in ["mix_pre", "mem_pre", "ff_pre", "mem_kv", "fox", "chk"]}
    GC = dict(mix_pre=0, mem_pre=8, ff_pre=16, mem_kv=24, fox=32, chk=36)
    bfbc = sb("bfbc", [128, 8], F32)
    cch = sb("cch", [128, 8], F32)
    B_bf = Buf("bfbc")
    B_cch = Buf("cch")
    E01 = sb("E01", [128, 8, 256], BF16)
    B_E01 = Buf("E01")
    wfg = sb("wfg", [128, 8, 8], BF16)
    B_wfg = Buf("wfg")

    fkT = sb("fkT", [128, 4, S_TOK], BF16)
    B_fkT = [Buf(f"fkT{t}") for t in range(16)]
    fv = sb("fv", [128, 16, 4, 3, 64], BF16)
    B_fv = [Buf(f"fv{t}") for t in range(16)]
    ckT = sb("ckT", [128, 4, 1024], BF16)
    B_ckT = [Buf(f"ckT{t}") for t in range(8)]
    cv = sb("cv", [128, 8, 4, 3, 64], BF16)
    B_cv = [Buf(f"cv{t}") for t in range(8)]
    c_all = sb("c_all", [128, 16, 8], F32)
    B_c = [Buf(f"c{t}") for t in range(16)]
    carry = sb("carry", [128, 17, 8], F32)
    B_carry = [Buf(f"carry{t}") for t in range(17)]
    biasI = [sb("biasI0", [128, 16, 8], F32)] * 2
    B_biasI = [Buf("biasI0")] * 2

    wsl = Rot([(sb(f"wsl{i}", [128, 8, 512], BF16), Buf(f"wsl{i}")) for i in range(NW)])
    xres = sb("xres", [128, 4, D], F32)
    B_xres = [Buf(f"xres{t}") for t in range(4)]
    actT = Rot([(sb(f"actT{i}", [128, 8, 512], BF16), [Buf(f"actT{i}_{t}") for t in range(4)]) for i in range(3)])
    f32p = Rot([(sb(f"f32p{i}", [128, 512], F32), Buf(f"f32p{i}")) for i in range(3)])
    xnp = Rot([(sb(f"xn{i}", [128, D], BF16), Buf(f"xn{i}")) for i in range(2)])
    stat = sb("stat", [128, 64], F32)
    statR = Rot([(stat[:, i:i + 1], Buf(f"stat{i}")) for i in range(64)])
    small8 = Rot([(sb(f"sm8_{i}", [128, 8], F32), Buf(f"sm8_{i}")) for i in range(8)])
    ptile = Rot([(sb(f"pt{i}", [128, 512], BF16), Buf(f"pt{i}")) for i in range(4)])
    y32 = sb("y32", [128, 4, 512], F32)
    B_y32 = [Buf(f"y32_{m}") for m in range(4)]
    arena = sb("arena", [128, 32, 512], BF16)
    B_ar = [Buf(f"ar{c}") for c in range(32)]
    kmT = sb("kmT", [128, 8, 256], BF16)
    vm = sb("vm", [128, 2, D], BF16)
    B_kmT = Buf("kmT")
    B_vm = Buf("vm")

    pb = [ps(f"pb{i}", [128, 512], F32) for i in range(6)]
    B_pb = [Buf(f"pb{i}") for i in range(6)]
    tpf = ps("tpf", [128, 512], F32)
    tp = tpf[:].bitcast(BF16).rearrange("p (c q) -> p c q", c=8)
    B_tp = Buf("tp")
    pm = ps("pm", [128, 512], F32)
    B_pm = Buf("pm")
    tp2 = pm[:].bitcast(BF16).rearrange("p (c q) -> p c q", c=8)
    tp_rot = Rot([(tp, B_tp), (tp2, B_pm)])
    bank = [(pb[i], B_pb[i]) for i in range(6)] + [(pm, B_pm), (tpf, B_tp)]
    rotAll = Rot([bank[i] for i in range(6)])

    def bcast_row(ap1d, n):
        return bass.AP(tensor=ap1d.tensor, offset=0, ap=[[0, 128], [1, n]])

    w_in_v = w_in_d.rearrange("(kc p) n -> p kc n", p=128)
    xresf = xres[:].rearrange("p a b -> p (a b)")
    R0 = xresf[:, 2048:3072].rearrange("p (h q) -> p h q", h=8)
    R1 = xresf[:, 3072:4096].rearrange("p (h q) -> p h q", h=8)
    B_R0 = B_xres[2]
    B_R1 = B_xres[3]
    y32f = y32[:].rearrange("p a b -> p (a b)")
    xstage = [(y32f[:, sl * 1024:(sl + 1) * 1024], [B_y32[2 * sl], B_y32[2 * sl + 1]]) for sl in range(2)]

    def gcol_load(k, d, n):
        src = bass.AP(tensor=d.tensor, offset=0, ap=[[1, 128], [128, n]])
        S.add("sp", (lambda e: e.dma_start(out=gcol[:, GC[k]:GC[k] + n], in_=src, allow_slow_non_contiguous=True)),
              writes=[B_gc[k]], dma=True)

    def setup_early():
        S.add("pool", lambda e: e.dma_start(out=cbf[:], in_=consts_d[:, 512:1024]), writes=[B_cbf], dma=True)
        gcol_load("mix_pre", g_mix_pre_d, 8)
        gcol_load("mem_kv", g_mem_kv_d, 8)
        S.add("pool", lambda e: e.dma_start(out=wfg[:], in_=w_in_v[:, :, 1536:1544]), writes=[B_wfg], dma=True)
        S.add("sp", lambda e: e.dma_start(out=cf32[:], in_=consts_d[:, 0:512]), writes=[B_cf32], dma=True)
        S.add("sp", lambda e: e.dma_start(out=bfbc[:], in_=bcast_row(b_fgt_d, 8)), writes=[B_bf], dma=True)
        S.add("dve", lambda e: e.memset(fv[:, :, :, 1, :], 1.0), writes=B_fv)
        S.add("dve", lambda e: e.memset(cv[:, :, :, 1, :], 1.0), writes=B_cv)
        S.add("dve", lambda e: e.memset(carry[:, 0, :], 0.0), writes=[B_carry[0]])

    def setup_late():
        gcol_load("mem_pre", g_mem_pre_d, 8)
        gcol_load("ff_pre", g_ff_pre_d, 8)
        gcol_load("fox", g_fox_d, 4)
        gcol_load("chk", g_chk_d, 4)
        S.add("sp", lambda e: e.dma_start(out=cch[:], in_=bass.AP(tensor=rel_d.tensor, offset=256, ap=[[0, 128], [257, 8]]),
                                          allow_slow_non_contiguous=True), writes=[B_cch], dma=True)
        for k, d in [("mix_post", g_mix_post_d), ("mem_post", g_mem_post_d), ("ff_post", g_ff_post_d)]:
            S.add("sp", (lambda e, k=k, d=d: e.dma_start(out=gbc[k][:], in_=bcast_row(d, D))), writes=[B_gbc[k]], dma=True)
        B_rb2 = Buf("rb2")
        B_rb2b = Buf("rb2b")
        S.add("sp", lambda e: e.dma_start(out=rb2_d[:, 0:8 * 257], in_=bcast_row(rel_d, 8 * 257)), writes=[B_rb2], dma=True)
        S.add("sp", lambda e: e.dma_start(out=rb2_d[:, 8 * 257:RBL], in_=bcast_row(rel_d, 128)), writes=[B_rb2b], dma=True)
        S.add("sp", lambda e: e.dma_start(out=R0, in_=bass.AP(tensor=rb2_d.tensor, offset=128,
                                                              ap=[[RBL - 1, 128], [257, 8], [1, 128]])),
              reads=[B_rb2, B_rb2b], writes=[B_R0], dma=True)
        S.add("sp", lambda e: e.dma_start(out=R1, in_=bass.AP(tensor=rb2_d.tensor, offset=256,
                                                              ap=[[RBL - 1, 128], [257, 8], [1, 128]])),
              reads=[B_rb2, B_rb2b], writes=[B_R1], dma=True)
        cch_b = cch[:, :].unsqueeze(2).to_broadcast([128, 8, 128])
        S.add("dve", lambda e: e.tensor_tensor(out=R0, in0=R0, in1=cch_b, op=ALU.subtract),
              reads=[B_R0, B_cch], writes=[B_R0])
        S.add("act", lambda e: e.activation(out=R0, in_=R0, func=AF.Exp), reads=[B_R0], writes=[B_R0])
        S.add("dve", lambda e: e.tensor_tensor(out=E01[:, :, 0:128], in0=R0,
                                               in1=m0_f.unsqueeze(1).to_broadcast([128, 8, 128]), op=ALU.mult),
              reads=[B_R0, B_cf32], writes=[B_E01])
        S.add("dve", lambda e: e.tensor_tensor(out=R1, in0=R1, in1=cch_b, op=ALU.subtract),
              reads=[B_R1, B_cch], writes=[B_R1])
        S.add("dve", lambda e: e.tensor_tensor(out=R1, in0=R1,
                                               in1=lt_f.unsqueeze(1).to_broadcast([128, 8, 128]), op=ALU.mult),
              reads=[B_R1, B_cf32], writes=[B_R1])
        S.add("act", lambda e: e.activation(out=E01[:, :, 128:256], in_=R1, func=AF.Exp), reads=[B_R1], writes=[B_E01])

    def dump(ap, bufs, row0, ncol):
        S.add("pool", (lambda e: e.dma_start(out=out_d[row0:row0 + 128, 0:ncol], in_=ap)), reads=bufs, dma=True)

    def finish():
        with es:
            S.emit(nc)
        return nc

    def load_w(src_ap):
        t, b = wsl.next()
        S.add("pool", (lambda e, t=t, src_ap=src_ap: e.dma_start(out=t[:], in_=src_ap)), writes=[b], dma=True)
        return t, b

    def wview(w_d):
        return w_d.rearrange("(kc p) n -> p kc n", p=128)

    def rms_rstd(srcs, src_bufs, dim, junk=None, junk_bufs=()):
        cols = []
        for s_ap in srcs:
            c, cb = statR.next()
            n = s_ap.shape[-1]
            if junk is None:
                jt, jb = ptile.next()
                jk, jbs = jt[:, 0:n], [jb]
            else:
                jk, jbs = junk, list(junk_bufs)
            S.add("act", (lambda e, s_ap=s_ap, c=c, jk=jk: e.activation(out=jk, in_=s_ap, func=AF.Square, accum_out=c)),
                  reads=src_bufs, writes=[cb] + jbs)
            cols.append((c, cb))
        if len(cols) == 2:
            c, cb = statR.next()
            S.add("dve", (lambda e, c=c, a=cols[0][0], b=cols[1][0]: e.tensor_tensor(out=c, in0=a, in1=b, op=ALU.add)),
                  reads=[cols[0][1], cols[1][1]], writes=[cb])
        else:
            c, cb = cols[0]
        l, lb = statR.next()
        S.add("act", (lambda e, l=l, c=c: e.activation(out=l, in_=c, func=AF.Ln, bias=EPS, scale=1.0 / dim)),
              reads=[cb], writes=[lb])
        r, rb = statR.next()
        S.add("act", (lambda e, r=r, l=l: e.activation(out=r, in_=l, func=AF.Exp, scale=-0.5)), reads=[lb], writes=[rb])
        return r, rb

    def prenorm_pre(src_ap, src_bufs):
        xn, xb = xnp.next()
        r, rb = rms_rstd([src_ap], src_bufs, D, junk=xn[:], junk_bufs=[xb])
        S.add("act", (lambda e: e.mul(out=xn[:], in_=src_ap, mul=r)),
              reads=list(src_bufs) + [rb], writes=[xb])
        return xn, xb

    def prenorm_tr(xn, xb, gname, dst, dst_buf, col0):
        tpx, tpb = tp_rot.next()

        def tr(e):
            last = None
            for c in range(8):
                last = e.transpose(tpx[:, c, :], xn[:, c * 128:(c + 1) * 128], ident_bf)
            return last
        S.add("pe", tr, reads=[xb, B_cbf], writes=[tpb])
        g0 = GC[gname]
        gb_ = gcol[:, g0:g0 + 8].unsqueeze(2).to_broadcast([128, 8, 128])
        S.add("dve", (lambda e: e.tensor_tensor(out=dst[:, :, col0:col0 + 128], in0=tpx, in1=gb_, op=ALU.mult)),
              reads=[tpb, B_gc[gname]], writes=[dst_buf])

    def prenorm_T(src_ap, src_bufs, gname, dst, dst_buf, col0):
        xn, xb = prenorm_pre(src_ap, src_bufs)
        prenorm_tr(xn, xb, gname, dst, dst_buf, col0)

    def mm_group(out_ap, pairs, reads, writes, first=True, last=True):
        def f(e):
            inst = None
            n = len(pairs)
            for i, (l, r) in enumerate(pairs):
                inst = e.matmul(out_ap, lhsT=l, rhs=r, start=(first and i == 0), stop=(last and i == n - 1))
            return inst
        S.add("pe", f, reads=reads, writes=writes)

    evac_tog = [0]

    def evac(out_ap, in_ap, reads, writes, scale=None, eng=None):
        evac_tog[0] ^= 1
        use_act = evac_tog[0] if eng is None else (eng == "act")
        if use_act:
            if scale is None:
                S.add("act", (lambda e: e.copy(out=out_ap, in_=in_ap)), reads=reads, writes=writes)
            else:
                S.add("act", (lambda e: e.mul(out=out_ap, in_=in_ap, mul=scale)), reads=reads, writes=writes)
        else:
            if scale is None:
                S.add("dve", (lambda e: e.tensor_copy(out=out_ap, in_=in_ap)), reads=reads, writes=writes)
            else:
                S.add("dve", (lambda e: e.tensor_scalar(out=out_ap, in0=in_ap, scalar1=scale, scalar2=None, op0=ALU.mult)),
                      reads=reads, writes=writes)

    def postnorm_residual(bA, bB, gname, t, also_out=None):
        r, rb = rms_rstd([bA[0], bB[0]], [bA[1], bB[1]], D)
        for half, (bap, bbuf) in enumerate([bA, bB]):
            tmp, tb = f32p.next()
            gsl = gbc[gname][:, half * 512:(half + 1) * 512]
            S.add("dve", (lambda e, tmp=tmp, bap=bap, gsl=gsl: e.scalar_tensor_tensor(
                out=tmp[:], in0=bap, scalar=r, in1=gsl, op0=ALU.mult, op1=ALU.mult)),
                reads=[bbuf, rb, B_gbc[gname]], writes=[tb])
            xs = xres[:, t, half * 512:(half + 1) * 512]
            S.add("dve", (lambda e, xs=xs, tmp=tmp: e.tensor_tensor(out=xs, in0=xs, in1=tmp[:], op=ALU.add)),
                  reads=[tb, B_xres[t]], writes=[B_xres[t]])

    def p1_prenorm(I_):
        hT_, B_hT_ = actT.next()
        for t in range(4):
            T = 4 * I_ + t
            xs_ap, xs_b = xstage[t % 2]
            S.add("sp", (lambda e, xs_ap=xs_ap, T=T: e.dma_start(out=xs_ap, in_=x_d[T * 128:(T + 1) * 128, :])),
                  writes=xs_b, dma=True)
            prenorm_T(xs_ap, xs_b, "mix_pre", hT_, B_hT_[t], t * 128)
        return hT_, B_hT_

    def load_xres(I_, t):
        T = 4 * I_ + t
        S.add("sp", (lambda e: e.dma_start(out=xres[:, t, :], in_=x_d[T * 128:(T + 1) * 128, :])),
              writes=[B_xres[t]], dma=True)

    setup_early()
    if stage == 0:
        setup_late()
        dump(E01[:].rearrange("p h q -> p (h q)")[:, 0:1024], [B_E01], 0, 1024)
        dump(E01[:].rearrange("p h q -> p (h q)")[:, 1024:2048], [B_E01], 128, 1024)
        dump(gcol[:], list(B_gc.values()), 256, 40)
        dump(cch[:], [B_cch], 384, 8)
        return finish()
    def out_proj_phase(srcT, B_srcT, wsl2, g_post, g_pre, dstT, B_dstT, early=None, B_late=None, late_banks=None):
        pre = {}
        for t in range(4):
            bks = []
            for h in range(2):
                if early is not None and t in early:
                    bkap, bb = early[t][h]
                    mm_group(bkap, [(srcT[:, k, t * 128:(t + 1) * 128], wsl2[h][0][:, k, :]) for k in range(4, 8)],
                             reads=list(B_late) + [wsl2[h][1]], writes=[bb], first=False, last=True)
                    bks.append((bkap, bb))
                    continue
                bk, bb = (late_banks[t][h] if late_banks is not None else rotAll.next())
                mm_group(bk[:, :], [(srcT[:, k, t * 128:(t + 1) * 128], wsl2[h][0][:, k, :]) for k in range(8)],
                         reads=list(B_srcT) + [wsl2[h][1]], writes=[bb])
                bks.append((bk[:, :], bb))
            postnorm_residual(bks[0], bks[1], g_post, t)
            if t == 2:
                pre[0] = prenorm_pre(xres[:, 0, :], [B_xres[0]])
        for t in range(4):
            if t not in pre:
                pre[t] = prenorm_pre(xres[:, t, :], [B_xres[t]])
            prenorm_tr(pre[t][0], pre[t][1], g_pre, dstT, B_dstT[t], t * 128)

    next_hT = p1_prenorm(0)

    def mem_kv_prep():
        memT, B_memT = actT.next()
        for mt in range(2):
            msl = xres[:, mt, :]
            mb_ = [B_xres[mt]]
            S.add("sp", (lambda e, msl=msl, mt=mt: e.dma_start(out=msl, in_=mem_d[mt * 128:(mt + 1) * 128, :])), writes=mb_, dma=True)
            prenorm_T(msl, mb_, "mem_kv", memT, B_memT[mt], mt * 128)
        setup_late()
        wk = [load_w(wview(w_mk_d)[:, :, h * 512:(h + 1) * 512]) for h in range(2)]
        for dc in range(8):
            wt, wb = wk[dc // 4]
            bk, bb = rotAll.next()
            mm_group(bk[:, 0:256], [(wt[:, k, (dc % 4) * 128:(dc % 4) * 128 + 128], memT[:, k, 0:256]) for k in range(8)],
                     reads=[wb, B_memT[0], B_memT[1]], writes=[bb])
            evac(kmT[:, dc, :], bk[:, 0:256], [bb], [B_kmT])
        wv_ = [load_w(wview(w_mv_d)[:, :, h * 512:(h + 1) * 512]) for h in range(2)]
        for mt in range(2):
            for h in range(2):
                wt, wb = wv_[h]
                bk, bb = rotAll.next()
                mm_group(bk[:, :], [(memT[:, k, mt * 128:(mt + 1) * 128], wt[:, k, :]) for k in range(8)],
                         reads=[wb, B_memT[mt]], writes=[bb])
                evac(vm[:, mt, h * 512:(h + 1) * 512], bk[:, :], [bb], [B_vm])

        for t in range(4):
            load_xres(0, t)

    if stage == 1:
        mem_kv_prep()
        kf = kmT[:].rearrange("p a b -> p (a b)")
        vf = vm[:].rearrange("p a b -> p (a b)")
        dump(kf[:, 0:1024], [B_kmT], 0, 1024)
        dump(kf[:, 1024:2048], [B_kmT], 128, 1024)
        dump(vf[:, 0:1024], [B_vm], 256, 1024)
        dump(vf[:, 1024:2048], [B_vm], 384, 1024)
        return finish()
    fq_sb = arena[:, 0:8, :]
    cq_sb = arena[:, 8:16, :]
    sq_sb = arena[:, 16:20, :]
    B_fq = B_ar[0:8]
    B_cq = B_ar[8:16]
    B_sq = B_ar[16:20]

    def evac_q(dst, B_dst, cc, bk, bb, eng=None):
        evac_tog[0] ^= 1
        if eng is None:
            eng = "act" if evac_tog[0] else "dve"
        for e_ in range(2):
            h = 2 * cc + e_
            rows = slice(64 * e_, 64 * e_ + 64)
            zrows = slice(64 * (1 - e_), 64 * (1 - e_) + 64)
            if eng == "act":
                S.add("act", (lambda e, h=h, rows=rows: e.mul(out=dst[rows, h, :], in_=bk[rows, :], mul=0.125)),
                      reads=[bb], writes=[B_dst[h]])
            else:
                S.add("dve", (lambda e, h=h, rows=rows: e.tensor_scalar(out=dst[rows, h, :], in0=bk[rows, :], scalar1=0.125,
                                                                         scalar2=None, op0=ALU.mult)),
                      reads=[bb], writes=[B_dst[h]])
            S.add("dve", (lambda e, h=h, zrows=zrows: e.memset(dst[zrows, h, :], 0.0)), writes=[B_dst[h]])
    Spool = Rot([bank[0], bank[1], bank[7]])
    Opool = Rot([bank[3], bank[4], bank[5], bank[6]])

    for I in range(NSB):
        tok0 = I * 512
        hT, B_hT = next_hT
        fl_tiles = []
        for t in range(4):
            mm_group(pm[:, t * 8:(t + 1) * 8], [(hT[:, k, t * 128:(t + 1) * 128], wfg[:, k, :]) for k in range(8)],
                     reads=[B_hT[t], B_wfg], writes=[B_pm])
        def flogit_chain():
            for t in range(4):
                fl, flb = small8.next()
                S.add("dve", (lambda e, fl=fl, t=t: e.tensor_tensor(out=fl[:], in0=pm[:, t * 8:(t + 1) * 8], in1=bfbc[:], op=ALU.add)),
                      reads=[B_pm, B_bf], writes=[flb])
                S.add("act", (lambda e, fl=fl: e.activation(out=fl[:], in_=fl[:], func=AF.Exp, scale=-1.0)), reads=[flb], writes=[flb])
                S.add("act", (lambda e, fl=fl: e.activation(out=fl[:], in_=fl[:], func=AF.Ln, bias=1.0, scale=1.0)), reads=[flb], writes=[flb])
                S.add("dve", (lambda e, fl=fl: e.tensor_scalar(out=fl[:], in0=fl[:], scalar1=-1.0, scalar2=None, op0=ALU.mult)),
                      reads=[flb], writes=[flb])
                fl_tiles.append((fl, flb))
        def cumsum_block():
            for t in range(4):
                fl, flb = fl_tiles[t]
                o0 = 32 + 16 * t
                S.add("pe", (lambda e, fl=fl, o0=o0: (e.matmul(pm[:, o0:o0 + 8], lhsT=tri_f, rhs=fl[:], start=True, stop=True),
                                                        e.matmul(pm[:, o0 + 8:o0 + 16], lhsT=ones_f, rhs=fl[:], start=True, stop=True))[1]),
                      reads=[flb, B_cf32], writes=[B_pm])
            for t in range(4):
                T = 4 * I + t
                o0 = 32 + 16 * t
                S.add("dve", (lambda e, T=T, o0=o0: e.tensor_tensor(out=c_all[:, T, :], in0=pm[:, o0:o0 + 8], in1=carry[:, T, :], op=ALU.add)),
                      reads=[B_pm, B_carry[T]], writes=[B_c[T]])
                S.add("dve", (lambda e, T=T, o0=o0: e.tensor_tensor(out=carry[:, T + 1, :], in0=pm[:, o0 + 8:o0 + 16], in1=carry[:, T, :], op=ALU.add)),
                      reads=[B_pm, B_carry[T]], writes=[B_carry[T + 1]])
        ring0 = (I % 2) * 512
        wcur = {}

        def proj_unit(name, c0, u, fill, k0=None, I=I, hT=hT, B_hT=B_hT, tok0=tok0, ring0=ring0, wcur=wcur):
            if u == 0 and (k0 is None or k0 == 0):
                wcur[name] = load_w(w_in_v[:, :, c0:c0 + 512])
            wt, wb = wcur[name]
            bk, bb = (bank[2] if fill else rotAll.next())
            eng = "dve" if fill else ("act" if I >= 1 else None)
            if name in ("fq", "fk", "cq", "ck"):
                cc = u
                pairs_ = [(wt[:, k, cc * 128:(cc + 1) * 128], hT[:, k, :]) for k in range(8)]
                reads_ = [wb] + B_hT
            else:
                pairs_ = [(hT[:, k, u * 128:(u + 1) * 128], wt[:, k, :]) for k in range(8)]
                reads_ = [wb, B_hT[u]]
            if k0 is not None:
                l_, r_ = pairs_[k0]
                S.add("pe", (lambda e: e.matmul(bk[:, :], lhsT=l_, rhs=r_, start=(k0 == 0), stop=(k0 == 7))),
                      reads=reads_, writes=[bb])
                if k0 < 7:
                    return
            elif True:
                mm_group(bk[:, :], pairs_, reads=reads_, writes=[bb])
            if name in ("fq", "fk", "cq", "ck"):
                if name == "fq":
                    evac_q(fq_sb, B_fq, cc, bk, bb, eng=eng)
                elif name == "cq":
                    evac_q(cq_sb, B_cq, cc, bk, bb, eng=eng)
                elif name == "fk":
                    evac(fkT[:, cc, tok0:tok0 + 512], bk[:, :], [bb], B_fkT[4 * I:4 * I + 4], eng=eng)
                else:
                    evac(ckT[:, cc, ring0:ring0 + 512], bk[:, :], [bb], [B_ckT[(4 * I + t) % 8] for t in range(4)], eng=eng)
            else:
                t = u
                T = 4 * I + t
                src = bk[:, :].rearrange("p (m e d) -> p m e d", m=4, e=2, d=64)
                if name == "fv":
                    evac(fv[:, T, :, 0:3:2, :], src, [bb], [B_fv[T]], eng=eng)
                else:
                    evac(cv[:, T % 8, :, 0:3:2, :], src, [bb], [B_cv[T % 8]], eng=eng)

        for name, c0 in [("fq", 0), ("fk", 512), ("fv", 1024)]:
            for u in range(4):
                proj_unit(name, c0, u, False)
            if name == "fq":
                flogit_chain()
            if name == "fv":
                cumsum_block()
        fillers = [(lambda name=name, c0=c0, u=u, k0=k0: proj_unit(name, c0, u, True, k0=k0))
                   for name, c0 in [("cq", 1544), ("ck", 2056), ("cv", 2568)] for u in range(4) for k0 in range(8)]
        nj = 4 * I + 4
        bI, bIb = biasI[I % 2], B_biasI[I % 2]
        S.add("dve", (lambda e, bI=bI, nj=nj: e.tensor_tensor(
            out=bI[:, 0:nj, :], in0=carry[:, nj - 2, :].unsqueeze(1).to_broadcast([128, nj, 8]),
            in1=c_all[:, 0:nj, :], op=ALU.subtract)),
            reads=[B_carry[nj - 2]] + B_c[0:nj], writes=[bIb])

        if I == 0:
            mem_kv_prep()
        if stage == 2:
            while fillers:
                fillers.pop(0)()
            for cc in range(4):
                dump(fq_sb[:, 2 * cc, :], [B_fq[2 * cc]], cc * 128, 512)
                dump(fkT[:, cc, 0:512], B_fkT[0:4], 512 + cc * 128, 512)
            dump(fv[:, 0].rearrange("p a b c -> p (a b c)"), [B_fv[0]], 1024, 768)
            dump(c_all[:].rearrange("p a b -> p (a b)"), B_c[0:4], 1152, 128)
            dump(bI[:].rearrange("p a b -> p (a b)"), [bIb], 1280, 128)
            dump(cv[:, 1].rearrange("p a b c -> p (a b c)"), [B_cv[1]], 1408, 768)
            for cc in range(4):
                dump(cq_sb[:, 2 * cc, :], [B_cq[2 * cc]], 1536 + cc * 64, 512)
            return finish()
        yT, B_yT = actT.next()
        B_yTf = [Buf(f"yTf{t}") for t in range(4)]
        gf_bank = [None]

        def attn_group(kind):
            for m in range(4):
                Ob = [Opool.next(), Opool.next()]
                items = []
                if kind == "fox":
                    for j in range(nj):
                        for e_ in range(2):
                            items.append((e_, j))
                else:
                    jl = list(range(max(0, 4 * I - 4), 4 * I + 4))
                    jstar = max(4 * I - 1, 0)
                    jl.remove(jstar)
                    jl = [jstar] + jl
                    for j in jl:
                        for e_ in range(2):
                            items.append((e_, j))
                nitems = len(items)
                pend = []

                def stage_a(idx):
                    e_, j = items[idx]
                    h = 2 * m + e_
                    pr = slice(64 * e_, 64 * e_ + 64)
                    Sb, Sbb = Spool.next()
                    Pt, Ptb = ptile.next()
                    if kind == "fox":
                        r_ = j - 4 * I
                        n0, n1 = max(r_, 0) * 128, 512
                        kap = fkT[:, m, j * 128:(j + 1) * 128]
                        qap = fq_sb[:, h, n0:n1]
                        mm_group(Sb[:, n0:n1], [(kap, qap)], reads=[B_fkT[j], B_fq[h]], writes=[Sbb])
                        S.add("act", (lambda e, Pt=Pt, Sb=Sb, n0=n0, n1=n1, j=j, h=h, bI=bI: e.activation(
                            out=Pt[:, n0:n1], in_=Sb[:, n0:n1], func=AF.Exp, bias=bI[:, j, h:h + 1], scale=1.0)),
                            reads=[Sbb, bIb], writes=[Ptb])
                        if r_ >= 0:
                            S.add("dve", (lambda e, Pt=Pt, n0=n0: e.tensor_tensor(
                                out=Pt[:, n0:n0 + 128], in0=Pt[:, n0:n0 + 128], in1=tri_bf, op=ALU.mult)),
                                reads=[Ptb, B_cbf], writes=[Ptb])
                        vap = fv[:, j, m, e_:e_ + 2, :].rearrange("p a d -> p (a d)")
                        vbuf = B_fv[j]
                    else:
                        a_lo, a_hi = max(j, 4 * I), min(j + 4, 4 * I + 3)
                        n0, n1 = (a_lo - 4 * I) * 128, (a_hi - 4 * I + 1) * 128
                        js = j % 8
                        kap = ckT[:, m, js * 128:(js + 1) * 128]
                        qap = cq_sb[:, h, n0:n1]
                        mm_group(Sb[:, n0:n1], [(kap, qap)], reads=[B_ckT[js], B_cq[h]], writes=[Sbb])
                        S.add("act", (lambda e, Pt=Pt, Sb=Sb, n0=n0, n1=n1, h=h: e.activation(
                            out=Pt[:, n0:n1], in_=Sb[:, n0:n1], func=AF.Exp, bias=cch[:, h:h + 1], scale=1.0)),
                            reads=[Sbb, B_cch], writes=[Ptb])
                        aa = [a for a in (j, j + 1) if a_lo <= a <= a_hi]
                        if aa:
                            c_ = (aa[0] - 4 * I) * 128
                            eo = (aa[0] - j) * 128
                            w_ = 128 * len(aa)
                            S.add("dve", (lambda e, Pt=Pt, c_=c_, eo=eo, h=h, w_=w_: e.tensor_tensor(
                                out=Pt[:, c_:c_ + w_], in0=Pt[:, c_:c_ + w_], in1=E01[:, h, eo:eo + w_], op=ALU.mult)),
                                reads=[Ptb, B_E01], writes=[Ptb])
                        a = j + 4
                        if a_lo <= a <= a_hi:
                            c_ = (a - 4 * I) * 128
                            S.add("dve", (lambda e, Pt=Pt, c_=c_: e.tensor_tensor(
                                out=Pt[:, c_:c_ + 128], in0=Pt[:, c_:c_ + 128], in1=m4_bf, op=ALU.mult)),
                                reads=[Ptb, B_cbf], writes=[Ptb])
                        vap = cv[:, js, m, e_:e_ + 2, :].rearrange("p a d -> p (a d)")
                        vbuf = B_cv[js]
                    pend.append((idx, e_, Pt, Ptb, n0, n1, vap, vbuf))

                first_seen = [True, True]
                last_idx = [max(i for i, it in enumerate(items) if it[0] == e_) for e_ in range(2)]

                def stage_b():
                    idx, e_, Pt, Ptb, n0, n1, vap, vbuf = pend.pop(0)
                    Ot, Otb = Ob[e_]
                    fst = first_seen[e_]
                    first_seen[e_] = False
                    mm_group(Ot[:, n0:n1], [(vap, Pt[:, n0:n1])], reads=[Ptb, vbuf], writes=[Otb],
                             first=fst, last=(idx == last_idx[e_]))

                LOOK = 3
                DEFER = 8
                G = gcount[0]
                gcount[0] += 1
                for idx in range(nitems):
                    stage_a(idx)
                    if idx >= LOOK:
                        stage_b()
                    if kind == "fox" and fillers:
                        fillers.pop(0)()
                    if idx == min(DEFER, nitems - 1):
                        while deferred and deferred[0][0] <= G:
                            deferred.pop(0)[1]()
                while pend:
                    stage_b()

                def norm_pair(m=m, Ob=Ob):
                    rd, rdb = f32p.next()
                    for e_ in range(2):
                        Ot, Otb = Ob[e_]
                        num = slice(64 * e_, 64 * e_ + 64)
                        den = slice(64 * (1 - e_), 64 * (1 - e_) + 64)
                        S.add("act", (lambda e, rd=rd, Ot=Ot, num=num, den=den: e.activation(out=rd[num, :], in_=Ot[den, :], func=AF.Ln)),
                              reads=[Otb], writes=[rdb])
                    S.add("act", (lambda e, rd=rd: e.activation(out=rd[:], in_=rd[:], func=AF.Exp, scale=-1.0)),
                          reads=[rdb], writes=[rdb])
                    for e_ in range(2):
                        Ot, Otb = Ob[e_]
                        num = slice(64 * e_, 64 * e_ + 64)
                        S.add("dve", (lambda e, rd=rd, Ot=Ot, num=num, m=m: e.tensor_tensor(
                            out=y32[num, m, :], in0=Ot[num, :], in1=rd[num, :], op=ALU.mult)),
                            reads=[Otb, rdb], writes=[B_y32[m]])

                def sq_pair(m=m):
                    S.add("dve", (lambda e, m=m: e.tensor_tensor(out=sq_sb[:, m, :], in0=y32[:, m, :], in1=y32[:, m, :], op=ALU.mult)),
                          reads=[B_y32[m]], writes=[B_sq[m]])
                deferred.append((G + 1, norm_pair))
                deferred.append((G + 2, sq_pair))

            def group_final(kind=kind):
                ssb, ssbb = (gf_bank[0] if (kind == "chk" and gf_bank[0] is not None) else Spool.next())
                mm_group(ssb[:, :], [(ones_bf, sq_sb[:, m, :]) for m in range(4)], reads=B_sq + [B_cbf], writes=[ssbb])
                lnv, lnb = f32p.next()
                S.add("act", (lambda e: e.activation(out=lnv[:], in_=ssb[:, :], func=AF.Ln, bias=EPS, scale=1.0 / 512)),
                      reads=[ssbb], writes=[lnb])
                S.add("act", (lambda e: e.activation(out=lnv[:], in_=lnv[:], func=AF.Exp, scale=-0.5)), reads=[lnb], writes=[lnb])
                g0 = GC[kind]
                off = 0 if kind == "fox" else 4
                for m in range(4):
                    S.add("dve", (lambda e, m=m, yT=yT, off=off, g0=g0, lnv=lnv: e.scalar_tensor_tensor(
                        out=yT[:, off + m, :], in0=y32[:, m, :], scalar=gcol[:, g0 + m:g0 + m + 1], in1=lnv[:],
                        op0=ALU.mult, op1=ALU.mult)),
                        reads=[B_y32[m], lnb, B_gc[kind]], writes=(B_yTf + B_yT if kind == "fox" else B_yT))
            deferred.append((gcount[0] + 1, group_final))

        deferred = []
        gcount = [0]
        attn_group("fox")
        while fillers:
            fillers.pop(0)()
        if stage == 3:
            while deferred:
                deferred.pop(0)[1]()
            for cc in range(4):
                dump(yT[:, cc, :], B_yT + B_yTf, cc * 128, 512)
            return finish()
        attn_group("chk")
        wo = [load_w(wview(w_out_d)[:, :, h * 512:(h + 1) * 512]) for h in range(2)]
        early = {}
        for t, bp in enumerate([(bank[0], bank[1]), (bank[2], bank[7])]):
            early[t] = []
            for h in range(2):
                bk, bb = bp[h]
                mm_group(bk[:, :], [(yT[:, k, t * 128:(t + 1) * 128], wo[h][0][:, k, :]) for k in range(4)],
                         reads=B_yTf + [wo[h][1]], writes=[bb], first=True, last=False)
                early[t].append((bk[:, :], bb))
        gf_bank[0] = bank[3]
        while deferred:
            deferred.pop(0)[1]()
        if stage == 4:
            for cc in range(8):
                dump(yT[:, cc, :], B_yT + B_yTf, cc * 128, 512)
            return finish()

        h2T, B_h2T = actT.next()
        out_proj_phase(yT, B_yT + B_yTf, wo, "mix_post", "mem_pre", h2T, B_h2T, early=early, B_late=B_yT,
                       late_banks={2: (bank[3], bank[4]), 3: (bank[5], bank[6])})

        if stage == 5:
            for t in range(4):
                dump(xres[:, t, :], [B_xres[t]], t * 128, 1024)
            for cc in range(8):
                dump(h2T[:, cc, :], B_h2T, 512 + cc * 128, 512)
            return finish()
        wq = [load_w(wview(w_mq_d)[:, :, h * 512:(h + 1) * 512]) for h in range(2)]
        qT, B_qT = actT.next()
        for dc in range(8):
            wt, wb = wq[dc // 4]
            bk, bb = rotAll.next()
            mm_group(bk[:, :], [(wt[:, k, (dc % 4) * 128:(dc % 4) * 128 + 128], h2T[:, k, :]) for k in range(8)],
                     reads=[wb] + B_h2T, writes=[bb])
            evac(qT[:, dc, :], bk[:, :], [bb], B_qT, scale=1.0 / 16)
        oT, B_oT = actT.next()
        def mem_S(hd):
            Ps = []
            for mt in range(2):
                Sb, Sbb = Spool.next()
                mm_group(Sb[:, :], [(kmT[:, 2 * hd + c, mt * 128:(mt + 1) * 128], qT[:, 2 * hd + c, :]) for c in range(2)],
                         reads=[B_kmT] + B_qT, writes=[Sbb])
                Pt, Ptb = ptile.next()
                S.add("act", (lambda e, Pt=Pt, Sb=Sb: e.activation(out=Pt[:], in_=Sb[:, :], func=AF.Exp)),
                      reads=[Sbb], writes=[Ptb])
                Ps.append((Pt, Ptb))
            return Ps

        def mem_PV(hd, Ps):
            dn, dnb = Opool.next()
            mm_group(dn[:, :], [(ones_bf, Ps[mt][0][:]) for mt in range(2)], reads=[Ps[0][1], Ps[1][1], B_cbf], writes=[dnb])
            rd, rdb = f32p.next()
            S.add("act", (lambda e, rd=rd, dn=dn: e.activation(out=rd[:], in_=dn[:, :], func=AF.Ln)), reads=[dnb], writes=[rdb])
            S.add("act", (lambda e, rd=rd: e.activation(out=rd[:], in_=rd[:], func=AF.Exp, scale=-1.0)), reads=[rdb], writes=[rdb])
            for c in range(2):
                Oc, Ocb = Opool.next()
                col = (2 * hd + c) * 128
                mm_group(Oc[:, :], [(vm[:, mt, col:col + 128], Ps[mt][0][:]) for mt in range(2)],
                         reads=[Ps[0][1], Ps[1][1], B_vm], writes=[Ocb])
                S.add("dve", (lambda e, Oc=Oc, rd=rd, hd=hd, c=c, oT=oT: e.tensor_tensor(
                    out=oT[:, 2 * hd + c, :], in0=Oc[:, :], in1=rd[:], op=ALU.mult)),
                    reads=[Ocb, rdb], writes=B_oT)

        prevP = None
        for hd in range(4):
            Ps = mem_S(hd)
            if prevP is not None:
                mem_PV(hd - 1, prevP)
            prevP = Ps
        mem_PV(3, prevP)
        wmo = [load_w(wview(w_mo_d)[:, :, h * 512:(h + 1) * 512]) for h in range(2)]
        h3T, B_h3T = actT.next()
        out_proj_phase(oT, B_oT, wmo, "mem_post", "ff_pre", h3T, B_h3T)

        if stage == 6:
            for t in range(4):
                dump(xres[:, t, :], [B_xres[t]], t * 128, 1024)
            return finish()
        U = Rot([bank[0], bank[1], bank[2]])
        for g in range(8):
            wt, wb = load_w(wview(w_ff1_d)[:, :, g * 512:(g + 1) * 512])
            for cc in range(4):
                ffc = 4 * g + cc
                bk, bb = U.next()
                mm_group(bk[:, :], [(wt[:, k, cc * 128:(cc + 1) * 128], h3T[:, k, :]) for k in range(8)],
                         reads=[wb] + B_h3T, writes=[bb])
                rl, rlb = f32p.next()
                S.add("act", (lambda e, rl=rl, bk=bk: e.activation(out=rl[:], in_=bk[:, :], func=AF.Relu)), reads=[bb], writes=[rlb])
                S.add("dve", (lambda e, rl=rl, ffc=ffc: e.tensor_tensor(out=arena[:, ffc, :], in0=rl[:], in1=rl[:], op=ALU.mult)),
                      reads=[rlb], writes=[B_ar[ffc]])
            if I + 1 < NSB and stage > 7 and g in (1, 2, 3, 4, 5):
                if g == 1:
                    nh = actT.next()
                    npre = {}

                def _stage(t, I=I, npre=npre):
                    T = 4 * (I + 1) + t
                    xs_ap, xs_b = xstage[t % 2]
                    S.add("sp", (lambda e, xs_ap=xs_ap, T=T: e.dma_start(out=xs_ap, in_=x_d[T * 128:(T + 1) * 128, :])),
                          writes=xs_b, dma=True)
                    npre[t] = prenorm_pre(xs_ap, xs_b)

                def _tr(t, nh=nh, npre=npre):
                    prenorm_tr(npre[t][0], npre[t][1], "mix_pre", nh[0], nh[1][t], t * 128)
                if g == 1:
                    _stage(0)
                elif g == 2:
                    _stage(1)
                elif g == 3:
                    _tr(0)
                    _tr(1)
                    _stage(2)
                elif g == 4:
                    _stage(3)
                else:
                    _tr(2)
                    _tr(3)
                    next_hT = nh
        if stage == 70:
            for cc in range(4):
                dump(arena[:, cc, :], [B_ar[cc]], cc * 128, 512)
            return finish()
        acc = [[bank[6], bank[7]], [bank[0], bank[1]], [bank[2], bank[3]], [bank[4], bank[5]]]
        w2v = wview(w_ff2_d)
        def ffn2_post(t, I=I, acc=acc):
            T = 4 * I + t
            postnorm_residual((acc[t][0][0][:, :], acc[t][0][1]), (acc[t][1][0][:, :], acc[t][1][1]), "ff_post", t)
            S.add("sp", (lambda e, t=t, T=T: e.dma_start(out=out_d[T * 128:(T + 1) * 128, :], in_=xres[:, t, :])),
                  reads=[B_xres[t]], dma=True)
            if I + 1 < NSB and stage > 7:
                load_xres(I + 1, t)

        for kg in range(4):
            wts = [load_w(w2v[:, kg * 8:(kg + 1) * 8, half * 512:(half + 1) * 512]) for half in range(2)]
            order = ([(half, t) for half in range(2) for t in range(4)] if kg < 3
                     else [(0, 0), (1, 0)] + [(half, t) for half in range(2) for t in range(1, 4)])
            for half, t in order:
                wt, wb = wts[half]
                mm_group(acc[t][half][0][:, :],
                         [(arena[:, kg * 8 + i, t * 128:(t + 1) * 128], wt[:, i, :]) for i in range(8)],
                         reads=[wb] + B_ar[kg * 8:(kg + 1) * 8], writes=[acc[t][half][1]],
                         first=(kg == 0), last=(kg == 3))
                if kg == 3 and (half, t) == (1, 0):
                    ffn2_post(0)
        for t in range(1, 4):
            ffn2_post(t)
        if stage == 7:
            return finish()
    return finish()


_NC_CACHE = {}


def kernel(**inputs):
    f32 = lambda a: np.ascontiguousarray(np.asarray(a, dtype=np.float32))
    x = f32(inputs["x"])
    mem = f32(inputs["mem"])
    B = x.shape[0]
    shared = {}
    for k in ["w_in", "w_out", "w_mq", "w_mk", "w_mv", "w_mo", "w_ff1", "w_ff2"]:
        shared[k] = f32(inputs[k])[0]
    for k in ["b_fgt", "g_fox_out", "g_chk_out", "g_mix_pre", "g_mix_post", "g_mem_kv",
              "g_mem_pre", "g_mem_post", "g_ff_pre", "g_ff_post"]:
        shared[k] = f32(inputs[k]).reshape(-1)
    shared["rel_bias"] = f32(inputs["rel_bias"]).reshape(-1)
    shared["consts"] = host_consts()
    if "nc" not in _NC_CACHE:
        _NC_CACHE["nc"] = build()
    nc = _NC_CACHE["nc"]
    in_maps = []
    for b in range(B):
        m = dict(shared)
        m["x"] = x[b]
        m["mem"] = mem[b]
        in_maps.append(m)
    res = run_bass_kernel_spmd(nc, in_maps, core_ids=list(range(B)))
    return np.stack([r["out"] for r in res.results], axis=0).astype(np.float32)
```

```python
import numpy as np
from contextlib import ExitStack
import concourse.bass as bass
import concourse.mybir as mybir
from concourse.bass_utils import run_bass_kernel_spmd

F32 = mybir.dt.float32
BF16 = mybir.dt.bfloat16
AF = mybir.ActivationFunctionType
ALU = mybir.AluOpType

ENGS = ["pe", "act", "dve", "pool", "sp"]
NDMASEM = 12
EPS = 1e-6
S_TOK = 2048
D = 1024
NSB = 4
D_IN = 3080
NW = 3


class Buf:
    __slots__ = ("name", "w", "rs")

    def __init__(self, name):
        self.name = name
        self.w = None
        self.rs = []


class Op:
    __slots__ = ("eng", "fn", "waits", "tok", "dma")


class Sched:
    def __init__(self):
        self.ops = {e: [] for e in ENGS}
        self.cnt = {e: 0 for e in ENGS}
        self.waited = {e: {} for e in ENGS}
        self.dma_n = {e: 0 for e in ENGS}

    def add(self, eng, fn, reads=(), writes=(), dma=False):
        deps = {}

        def need(tok):
            if tok is None:
                return
            k, v = tok
            if k == ("e", "pe") and eng == "pe" and not dma:
                return
            if deps.get(k, 0) < v:
                deps[k] = v

        for b in reads:
            need(b.w)
        for b in writes:
            need(b.w)
            for t in b.rs:
                need(t)
        op = Op()
        op.eng = eng
        op.fn = fn
        op.dma = dma
        if dma:
            n = self.dma_n[eng]
            self.dma_n[eng] = n + 1
            slot = n % NDMASEM
            use = n // NDMASEM
            key = ("d", eng, slot)
            if use > 0:
                need((key, 16 * use))
            op.tok = (key, 16 * (use + 1))
        else:
            self.cnt[eng] += 1
            op.tok = (("e", eng), self.cnt[eng])
        w = self.waited[eng]
        waits = []
        for k, v in deps.items():
            if w.get(k, 0) >= v:
                continue
            w[k] = v
            waits.append((k, v))
        op.waits = waits
        self.ops[eng].append(op)
        for b in reads:
            b.rs.append(op.tok)
        for b in writes:
            b.w = op.tok
            b.rs = []
        return op.tok

    def emit(self, nc, final_wait_eng="sp"):
        with ExitStack() as es:
            sems = {}
            for e in ENGS:
                sems[("e", e)] = es.enter_context(nc.semaphore("s_" + e))
                for i in range(min(NDMASEM, self.dma_n[e])):
                    sems[("d", e, i)] = es.enter_context(nc.semaphore(f"d_{e}_{i}"))
            finals = []
            for e in ENGS:
                if self.cnt[e]:
                    finals.append((("e", e), self.cnt[e]))
                n = self.dma_n[e]
                for i in range(min(NDMASEM, n)):
                    uses = (n - 1 - i) // NDMASEM + 1
                    finals.append((("d", e, i), 16 * uses))
            block = es.enter_context(nc.Block())

            def run(e, h):
                for op in self.ops[e]:
                    for k, v in op.waits:
                        h.wait_ge(sems[k], v)
                    inst = op.fn(h)
                    k, v = op.tok
                    inst.then_inc(sems[k], 16 if op.dma else 1)
                if e == final_wait_eng:
                    for k, v in finals:
                        h.wait_ge(sems[k], v)

            @block.tensor
            def _(t):
                run("pe", t)

            @block.scalar
            def _(t):
                run("act", t)

            @block.vector
            def _(t):
                run("dve", t)

            @block.gpsimd
            def _(t):
                run("pool", t)

            @block.sync
            def _(t):
                run("sp", t)


class Rot:
    def __init__(self, items):
        self.items = items
        self.i = 0

    def next(self):
        it = self.items[self.i % len(self.items)]
        self.i += 1
        return it


def host_consts():
    c = np.zeros((128, 1024), np.float32)
    i = np.arange(128)
    tri = (i[:, None] <= i[None, :])
    c[:, 0:128] = tri
    c[:, 128:256] = (i[None, :] <= i[:, None])
    c[:, 256:384] = 1.0
    c[:, 384:512] = 1.0 - ((i[:, None] >= 64) & (i[None, :] < 64))
    c[:, 512:640] = np.eye(128)
    c[:, 640:768] = tri
    c[:, 768:896] = 1.0
    c[:, 896:1024] = 1.0 - ((i[:, None] < 64) & (i[None, :] >= 64))
    return c


def build(stage=99):
    nc = bass.Bass("TRN2", target_bir_lowering=False)

    def din(name, shape):
        return nc.dram_tensor(name, shape, F32, kind="ExternalInput").ap()

    x_d = din("x", [S_TOK, D])
    mem_d = din("mem", [256, D])
    w_in_d = din("w_in", [D, D_IN])
    b_fgt_d = din("b_fgt", [8])
    rel_d = din("rel_bias", [8 * 257])
    g_fox_d = din("g_fox_out", [512])
    g_chk_d = din("g_chk_out", [512])
    w_out_d = din("w_out", [D, D])
    g_mix_pre_d = din("g_mix_pre", [D])
    g_mix_post_d = din("g_mix_post", [D])
    g_mem_kv_d = din("g_mem_kv", [D])
    w_mq_d = din("w_mq", [D, D])
    w_mk_d = din("w_mk", [D, D])
    w_mv_d = din("w_mv", [D, D])
    w_mo_d = din("w_mo", [D, D])
    g_mem_pre_d = din("g_mem_pre", [D])
    g_mem_post_d = din("g_mem_post", [D])
    w_ff1_d = din("w_ff1", [D, 4 * D])
    w_ff2_d = din("w_ff2", [4 * D, D])
    g_ff_pre_d = din("g_ff_pre", [D])
    g_ff_post_d = din("g_ff_post", [D])
    consts_d = din("consts", [128, 1024])
    out_d = nc.dram_tensor("out", [S_TOK, D], F32, kind="ExternalOutput").ap()
    RBL = 8 * 257 + 128
    rb2_d = nc.dram_tensor("rb2", [128, RBL], F32, kind="Internal").ap()

    S = Sched()
    es = ExitStack()

    def sb(name, shape, dt):
        return es.enter_context(nc.sbuf_tensor(name, shape, dt))

    def ps(name, shape, dt):
        return es.enter_context(nc.psum_tensor(name, shape, dt))

    cf32 = sb("cf32", [128, 512], F32)
    cbf = sb("cbf", [128, 512], BF16)
    ident_bf = cbf[:, 0:128]
    tri_bf = cbf[:, 128:256]
    ones_bf = cbf[:, 256:384]
    m4_bf = cbf[:, 384:512]
    tri_f = cf32[:, 0:128]
    lt_f = cf32[:, 128:256]
    ones_f = cf32[:, 256:384]
    m0_f = cf32[:, 384:512]
    B_cf32 = Buf("cf32")
    B_cbf = Buf("cbf")
    gbc = {k: sb("gbc_" + k, [128, D], F32) for k in ["mix_post", "mem_post", "ff_post"]}
    B_gbc = {k: Buf("gbc_" + k) for k in gbc}
    gcol = sb("gcol", [128, 40], F32)
    B_gc = {k: Buf("gcol_" + k) for k in ["mix_pre", "mem_pre", "ff_pre", "mem_kv", "fox", "chk"]}
    GC = dict(mix_pre=0, mem_pre=8, ff_pre=16, mem_kv=24, fox=32, chk=36)
    bfbc = sb("bfbc", [128, 8], F32)
    cch = sb("cch", [128, 8], F32)
    B_bf = Buf("bfbc")
    B_cch = Buf("cch")
    E01 = sb("E01", [128, 8, 256], BF16)
    B_E01 = Buf("E01")
    wfg = sb("wfg", [128, 8, 8], BF16)
    B_wfg = Buf("wfg")

    fkT = sb("fkT", [128, 4, S_TOK], BF16)
    B_fkT = [Buf(f"fkT{t}") for t in range(16)]
    fv = sb("fv", [128, 16, 4, 3, 64], BF16)
    B_fv = [Buf(f"fv{t}") for t in range(16)]
    ckT = sb("ckT", [128, 4, 1024], BF16)
    B_ckT = [Buf(f"ckT{t}") for t in range(8)]
    cv = sb("cv", [128, 8, 4, 3, 64], BF16)
    B_cv = [Buf(f"cv{t}") for t in range(8)]
    c_all = sb("c_all", [128, 16, 8], F32)
    B_c = [Buf(f"c{t}") for t in range(16)]
    carry = sb("carry", [128, 17, 8], F32)
    B_carry = [Buf(f"carry{t}") for t in range(17)]
    biasI = [sb("biasI0", [128, 16, 8], F32)] * 2
    B_biasI = [Buf("biasI0")] * 2

    wsl = Rot([(sb(f"wsl{i}", [128, 8, 512], BF16), Buf(f"wsl{i}")) for i in range(NW)])
    xres = sb("xres", [128, 4, D], F32)
    B_xres = [Buf(f"xres{t}") for t in range(4)]
    actT = Rot([(sb(f"actT{i}", [128, 8, 512], BF16), [Buf(f"actT{i}_{t}") for t in range(4)]) for i in range(3)])
    f32p = Rot([(sb(f"f32p{i}", [128, 512], F32), Buf(f"f32p{i}")) for i in range(3)])
    xnp = Rot([(sb(f"xn{i}", [128, D], BF16), Buf(f"xn{i}")) for i in range(2)])
    stat = sb("stat", [128, 64], F32)
    statR = Rot([(stat[:, i:i + 1], Buf(f"stat{i}")) for i in range(64)])
    small8 = Rot([(sb(f"sm8_{i}", [128, 8], F32), Buf(f"sm8_{i}")) for i in range(8)])
    ptile = Rot([(sb(f"pt{i}", [128, 512], BF16), Buf(f"pt{i}")) for i in range(4)])
    y32 = sb("y32", [128, 4, 512], F32)
    B_y32 = [Buf(f"y32_{m}") for m in range(4)]
    arena = sb("arena", [128, 32, 512], BF16)
    B_ar = [Buf(f"ar{c}") for c in range(32)]
    kmT = sb("kmT", [128, 8, 256], BF16)
    vm = sb("vm", [128, 2, D], BF16)
    B_kmT = Buf("kmT")
    B_vm = Buf("vm")

    pb = [ps(f"pb{i}", [128, 512], F32) for i in range(6)]
    B_pb = [Buf(f"pb{i}") for i in range(6)]
    tpf = ps("tpf", [128, 512], F32)
    tp = tpf[:].bitcast(BF16).rearrange("p (c q) -> p c q", c=8)
    B_tp = Buf("tp")
    pm = ps("pm", [128, 512], F32)
    B_pm = Buf("pm")
    tp2 = pm[:].bitcast(BF16).rearrange("p (c q) -> p c q", c=8)
    tp_rot = Rot([(tp, B_tp), (tp2, B_pm)])
    bank = [(pb[i], B_pb[i]) for i in range(6)] + [(pm, B_pm), (tpf, B_tp)]
    rotAll = Rot([bank[i] for i in range(6)])

    def bcast_row(ap1d, n):
        return bass.AP(tensor=ap1d.tensor, offset=0, ap=[[0, 128], [1, n]])

    w_in_v = w_in_d.rearrange("(kc p) n -> p kc n", p=128)
    xresf = xres[:].rearrange("p a b -> p (a b)")
    R0 = xresf[:, 2048:3072].rearrange("p (h q) -> p h q", h=8)
    R1 = xresf[:, 3072:4096].rearrange("p (h q) -> p h q", h=8)
    B_R0 = B_xres[2]
    B_R1 = B_xres[3]
    y32f = y32[:].rearrange("p a b -> p (a b)")
    xstage = [(y32f[:, sl * 1024:(sl + 1) * 1024], [B_y32[2 * sl], B_y32[2 * sl + 1]]) for sl in range(2)]

    def gcol_load(k, d, n):
        src = bass.AP(tensor=d.tensor, offset=0, ap=[[1, 128], [128, n]])
        S.add("sp", (lambda e: e.dma_start(out=gcol[:, GC[k]:GC[k] + n], in_=src, allow_slow_non_contiguous=True)),
              writes=[B_gc[k]], dma=True)

    def setup_early():
        S.add("pool", lambda e: e.dma_start(out=cbf[:], in_=consts_d[:, 512:1024]), writes=[B_cbf], dma=True)
        gcol_load("mix_pre", g_mix_pre_d, 8)
        gcol_load("mem_kv", g_mem_kv_d, 8)
        S.add("pool", lambda e: e.dma_start(out=wfg[:], in_=w_in_v[:, :, 1536:1544]), writes=[B_wfg], dma=True)
        S.add("sp", lambda e: e.dma_start(out=cf32[:], in_=consts_d[:, 0:512]), writes=[B_cf32], dma=True)
        S.add("sp", lambda e: e.dma_start(out=bfbc[:], in_=bcast_row(b_fgt_d, 8)), writes=[B_bf], dma=True)
        S.add("dve", lambda e: e.memset(fv[:, :, :, 1, :], 1.0), writes=B_fv)
        S.add("dve", lambda e: e.memset(cv[:, :, :, 1, :], 1.0), writes=B_cv)
        S.add("dve", lambda e: e.memset(carry[:, 0, :], 0.0), writes=[B_carry[0]])

    def setup_late():
        gcol_load("mem_pre", g_mem_pre_d, 8)
        gcol_load("ff_pre", g_ff_pre_d, 8)
        gcol_load("fox", g_fox_d, 4)
        gcol_load("chk", g_chk_d, 4)
        S.add("sp", lambda e: e.dma_start(out=cch[:], in_=bass.AP(tensor=rel_d.tensor, offset=256, ap=[[0, 128], [257, 8]]),
                                          allow_slow_non_contiguous=True), writes=[B_cch], dma=True)
        for k, d in [("mix_post", g_mix_post_d), ("mem_post", g_mem_post_d), ("ff_post", g_ff_post_d)]:
            S.add("sp", (lambda e, k=k, d=d: e.dma_start(out=gbc[k][:], in_=bcast_row(d, D))), writes=[B_gbc[k]], dma=True)
        B_rb2 = Buf("rb2")
        B_rb2b = Buf("rb2b")
        S.add("sp", lambda e: e.dma_start(out=rb2_d[:, 0:8 * 257], in_=bcast_row(rel_d, 8 * 257)), writes=[B_rb2], dma=True)
        S.add("sp", lambda e: e.dma_start(out=rb2_d[:, 8 * 257:RBL], in_=bcast_row(rel_d, 128)), writes=[B_rb2b], dma=True)
        S.add("sp", lambda e: e.dma_start(out=R0, in_=bass.AP(tensor=rb2_d.tensor, offset=128,
                                                              ap=[[RBL - 1, 128], [257, 8], [1, 128]])),
              reads=[B_rb2, B_rb2b], writes=[B_R0], dma=True)
        S.add("sp", lambda e: e.dma_start(out=R1, in_=bass.AP(tensor=rb2_d.tensor, offset=256,
                                                              ap=[[RBL - 1, 128], [257, 8], [1, 128]])),
              reads=[B_rb2, B_rb2b], writes=[B_R1], dma=True)
        cch_b = cch[:, :].unsqueeze(2).to_broadcast([128, 8, 128])
        S.add("dve", lambda e: e.tensor_tensor(out=R0, in0=R0, in1=cch_b, op=ALU.subtract),
              reads=[B_R0, B_cch], writes=[B_R0])
        S.add("act", lambda e: e.activation(out=R0, in_=R0, func=AF.Exp), reads=[B_R0], writes=[B_R0])
        S.add("dve", lambda e: e.tensor_tensor(out=E01[:, :, 0:128], in0=R0,
                                               in1=m0_f.unsqueeze(1).to_broadcast([128, 8, 128]), op=ALU.mult),
              reads=[B_R0, B_cf32], writes=[B_E01])
        S.add("dve", lambda e: e.tensor_tensor(out=R1, in0=R1, in1=cch_b, op=ALU.subtract),
              reads=[B_R1, B_cch], writes=[B_R1])
        S.add("dve", lambda e: e.tensor_tensor(out=R1, in0=R1,
                                               in1=lt_f.unsqueeze(1).to_broadcast([128, 8, 128]), op=ALU.mult),
              reads=[B_R1, B_cf32], writes=[B_R1])
        S.add("act", lambda e: e.activation(out=E01[:, :, 128:256], in_=R1, func=AF.Exp), reads=[B_R1], writes=[B_E01])

    def dump(ap, bufs, row0, ncol):
        S.add("pool", (lambda e: e.dma_start(out=out_d[row0:row0 + 128, 0:ncol], in_=ap)), reads=bufs, dma=True)

    def finish():
        with es:
            S.emit(nc)
        return nc

    def load_w(src_ap):
        t, b = wsl.next()
        S.add("pool", (lambda e, t=t, src_ap=src_ap: e.dma_start(out=t[:], in_=src_ap)), writes=[b], dma=True)
        return t, b

    def wview(w_d):
        return w_d.rearrange("(kc p) n -> p kc n", p=128)

    def rms_rstd(srcs, src_bufs, dim, junk=None, junk_bufs=()):
        cols = []
        for s_ap in srcs:
            c, cb = statR.next()
            n = s_ap.shape[-1]
            if junk is None:
                jt, jb = ptile.next()
                jk, jbs = jt[:, 0:n], [jb]
            else:
                jk, jbs = junk, list(junk_bufs)
            S.add("act", (lambda e, s_ap=s_ap, c=c, jk=jk: e.activation(out=jk, in_=s_ap, func=AF.Square, accum_out=c)),
                  reads=src_bufs, writes=[cb] + jbs)
            cols.append((c, cb))
        if len(cols) == 2:
            c, cb = statR.next()
            S.add("dve", (lambda e, c=c, a=cols[0][0], b=cols[1][0]: e.tensor_tensor(out=c, in0=a, in1=b, op=ALU.add)),
                  reads=[cols[0][1], cols[1][1]], writes=[cb])
        else:
            c, cb = cols[0]
        l, lb = statR.next()
        S.add("act", (lambda e, l=l, c=c: e.activation(out=l, in_=c, func=AF.Ln, bias=EPS, scale=1.0 / dim)),
              reads=[cb], writes=[lb])
        r, rb = statR.next()
        S.add("act", (lambda e, r=r, l=l: e.activation(out=r, in_=l, func=AF.Exp, scale=-0.5)), reads=[lb], writes=[rb])
        return r, rb

    def prenorm_pre(src_ap, src_bufs):
        xn, xb = xnp.next()
        r, rb = rms_rstd([src_ap], src_bufs, D, junk=xn[:], junk_bufs=[xb])
        S.add("act", (lambda e: e.mul(out=xn[:], in_=src_ap, mul=r)),
              reads=list(src_bufs) + [rb], writes=[xb])
        return xn, xb

    def prenorm_tr(xn, xb, gname, dst, dst_buf, col0):
        tpx, tpb = tp_rot.next()

        def tr(e):
            last = None
            for c in range(8):
                last = e.transpose(tpx[:, c, :], xn[:, c * 128:(c + 1) * 128], ident_bf)
            return last
        S.add("pe", tr, reads=[xb, B_cbf], writes=[tpb])
        g0 = GC[gname]
        gb_ = gcol[:, g0:g0 + 8].unsqueeze(2).to_broadcast([128, 8, 128])
        S.add("dve", (lambda e: e.tensor_tensor(out=dst[:, :, col0:col0 + 128], in0=tpx, in1=gb_, op=ALU.mult)),
              reads=[tpb, B_gc[gname]], writes=[dst_buf])

    def prenorm_T(src_ap, src_bufs, gname, dst, dst_buf, col0):
        xn, xb = prenorm_pre(src_ap, src_bufs)
        prenorm_tr(xn, xb, gname, dst, dst_buf, col0)

    def mm_group(out_ap, pairs, reads, writes, first=True, last=True):
        def f(e):
            inst = None
            n = len(pairs)
            for i, (l, r) in enumerate(pairs):
                inst = e.matmul(out_ap, lhsT=l, rhs=r, start=(first and i == 0), stop=(last and i == n - 1))
            return inst
        S.add("pe", f, reads=reads, writes=writes)

    evac_tog = [0]

    def evac(out_ap, in_ap, reads, writes, scale=None, eng=None):
        evac_tog[0] ^= 1
        use_act = evac_tog[0] if eng is None else (eng == "act")
        if use_act:
            if scale is None:
                S.add("act", (lambda e: e.copy(out=out_ap, in_=in_ap)), reads=reads, writes=writes)
            else:
                S.add("act", (lambda e: e.mul(out=out_ap, in_=in_ap, mul=scale)), reads=reads, writes=writes)
        else:
            if scale is None:
                S.add("dve", (lambda e: e.tensor_copy(out=out_ap, in_=in_ap)), reads=reads, writes=writes)
            else:
                S.add("dve", (lambda e: e.tensor_scalar(out=out_ap, in0=in_ap, scalar1=scale, scalar2=None, op0=ALU.mult)),
                      reads=reads, writes=writes)

    def postnorm_residual(bA, bB, gname, t, also_out=None):
        r, rb = rms_rstd([bA[0], bB[0]], [bA[1], bB[1]], D)
        for half, (bap, bbuf) in enumerate([bA, bB]):
            tmp, tb = f32p.next()
            gsl = gbc[gname][:, half * 512:(half + 1) * 512]
            S.add("dve", (lambda e, tmp=tmp, bap=bap, gsl=gsl: e.scalar_tensor_tensor(
                out=tmp[:], in0=bap, scalar=r, in1=gsl, op0=ALU.mult, op1=ALU.mult)),
                reads=[bbuf, rb, B_gbc[gname]], writes=[tb])
            xs = xres[:, t, half * 512:(half + 1) * 512]
            S.add("dve", (lambda e, xs=xs, tmp=tmp: e.tensor_tensor(out=xs, in0=xs, in1=tmp[:], op=ALU.add)),
                  reads=[tb, B_xres[t]], writes=[B_xres[t]])

    def p1_prenorm(I_):
        hT_, B_hT_ = actT.next()
        for t in range(4):
            T = 4 * I_ + t
            xs_ap, xs_b = xstage[t % 2]
            S.add("sp", (lambda e, xs_ap=xs_ap, T=T: e.dma_start(out=xs_ap, in_=x_d[T * 128:(T + 1) * 128, :])),
                  writes=xs_b, dma=True)
            prenorm_T(xs_ap, xs_b, "mix_pre", hT_, B_hT_[t], t * 128)
        return hT_, B_hT_

    def load_xres(I_, t):
        T = 4 * I_ + t
        S.add("sp", (lambda e: e.dma_start(out=xres[:, t, :], in_=x_d[T * 128:(T + 1) * 128, :])),
              writes=[B_xres[t]], dma=True)

    setup_early()
    if stage == 0:
        setup_late()
        dump(E01[:].rearrange("p h q -> p (h q)")[:, 0:1024], [B_E01], 0, 1024)
        dump(E01[:].rearrange("p h q -> p (h q)")[:, 1024:2048], [B_E01], 128, 1024)
        dump(gcol[:], list(B_gc.values()), 256, 40)
        dump(cch[:], [B_cch], 384, 8)
        return finish()
    def out_proj_phase(srcT, B_srcT, wsl2, g_post, g_pre, dstT, B_dstT, early=None, B_late=None, late_banks=None):
        pre = {}
        for t in range(4):
            bks = []
            for h in range(2):
                if early is not None and t in early:
                    bkap, bb = early[t][h]
                    for k in range(4, 8):
                        mm_group(bkap, [(srcT[:, k, t * 128:(t + 1) * 128], wsl2[h][0][:, k, :])],
                                 reads=[B_late[k - 4], wsl2[h][1]], writes=[bb], first=False, last=(k == 7))
                    bks.append((bkap, bb))
                    continue
                bk, bb = (late_banks[t][h] if late_banks is not None else rotAll.next())
                mm_group(bk[:, :], [(srcT[:, k, t * 128:(t + 1) * 128], wsl2[h][0][:, k, :]) for k in range(8)],
                         reads=list(B_srcT) + [wsl2[h][1]], writes=[bb])
                bks.append((bk[:, :], bb))
            postnorm_residual(bks[0], bks[1], g_post, t)
            if t == 2:
                pre[0] = prenorm_pre(xres[:, 0, :], [B_xres[0]])
        for t in range(4):
            if t not in pre:
                pre[t] = prenorm_pre(xres[:, t, :], [B_xres[t]])
            prenorm_tr(pre[t][0], pre[t][1], g_pre, dstT, B_dstT[t], t * 128)

    next_hT = p1_prenorm(0)

    def mem_kv_prep():
        memT, B_memT = actT.next()
        for mt in range(2):
            msl = xres[:, mt, :]
            mb_ = [B_xres[mt]]
            S.add("sp", (lambda e, msl=msl, mt=mt: e.dma_start(out=msl, in_=mem_d[mt * 128:(mt + 1) * 128, :])), writes=mb_, dma=True)
            prenorm_T(msl, mb_, "mem_kv", memT, B_memT[mt], mt * 128)
        setup_late()
        wk = [load_w(wview(w_mk_d)[:, :, h * 512:(h + 1) * 512]) for h in range(2)]
        for dc in range(8):
            wt, wb = wk[dc // 4]
            bk, bb = rotAll.next()
            mm_group(bk[:, 0:256], [(wt[:, k, (dc % 4) * 128:(dc % 4) * 128 + 128], memT[:, k, 0:256]) for k in range(8)],
                     reads=[wb, B_memT[0], B_memT[1]], writes=[bb])
            evac(kmT[:, dc, :], bk[:, 0:256], [bb], [B_kmT])
        wv_ = [load_w(wview(w_mv_d)[:, :, h * 512:(h + 1) * 512]) for h in range(2)]
        for mt in range(2):
            for h in range(2):
                wt, wb = wv_[h]
                bk, bb = rotAll.next()
                mm_group(bk[:, :], [(memT[:, k, mt * 128:(mt + 1) * 128], wt[:, k, :]) for k in range(8)],
                         reads=[wb, B_memT[mt]], writes=[bb])
                evac(vm[:, mt, h * 512:(h + 1) * 512], bk[:, :], [bb], [B_vm])

        for t in range(4):
            load_xres(0, t)

    if stage == 1:
        mem_kv_prep()
        kf = kmT[:].rearrange("p a b -> p (a b)")
        vf = vm[:].rearrange("p a b -> p (a b)")
        dump(kf[:, 0:1024], [B_kmT], 0, 1024)
        dump(kf[:, 1024:2048], [B_kmT], 128, 1024)
        dump(vf[:, 0:1024], [B_vm], 256, 1024)
        dump(vf[:, 1024:2048], [B_vm], 384, 1024)
        return finish()
    fq_sb = arena[:, 0:8, :]
    cq_sb = arena[:, 8:16, :]
    sq_sb = arena[:, 16:20, :]
    B_fq = B_ar[0:8]
    B_cq = B_ar[8:16]
    B_sq = B_ar[16:20]

    def evac_q(dst, B_dst, cc, bk, bb, eng=None):
        evac_tog[0] ^= 1
        if eng is None:
            eng = "act" if evac_tog[0] else "dve"
        for e_ in range(2):
            h = 2 * cc + e_
            rows = slice(64 * e_, 64 * e_ + 64)
            zrows = slice(64 * (1 - e_), 64 * (1 - e_) + 64)
            if eng == "act":
                S.add("act", (lambda e, h=h, rows=rows: e.mul(out=dst[rows, h, :], in_=bk[rows, :], mul=0.125)),
                      reads=[bb], writes=[B_dst[h]])
            else:
                S.add("dve", (lambda e, h=h, rows=rows: e.tensor_scalar(out=dst[rows, h, :], in0=bk[rows, :], scalar1=0.125,
                                                                         scalar2=None, op0=ALU.mult)),
                      reads=[bb], writes=[B_dst[h]])
            S.add("dve", (lambda e, h=h, zrows=zrows: e.memset(dst[zrows, h, :], 0.0)), writes=[B_dst[h]])
    Spool = Rot([bank[0], bank[1], bank[7]])
    Opool = Rot([bank[3], bank[4], bank[5], bank[6]])

    for I in range(NSB):
        tok0 = I * 512
        hT, B_hT = next_hT
        fl_tiles = []
        for t in range(4):
            mm_group(pm[:, t * 8:(t + 1) * 8], [(hT[:, k, t * 128:(t + 1) * 128], wfg[:, k, :]) for k in range(8)],
                     reads=[B_hT[t], B_wfg], writes=[B_pm])
        def flogit_chain():
            for t in range(4):
                fl, flb = small8.next()
                S.add("dve", (lambda e, fl=fl, t=t: e.tensor_tensor(out=fl[:], in0=pm[:, t * 8:(t + 1) * 8], in1=bfbc[:], op=ALU.add)),
                      reads=[B_pm, B_bf], writes=[flb])
                S.add("act", (lambda e, fl=fl: e.activation(out=fl[:], in_=fl[:], func=AF.Exp, scale=-1.0)), reads=[flb], writes=[flb])
                S.add("act", (lambda e, fl=fl: e.activation(out=fl[:], in_=fl[:], func=AF.Ln, bias=1.0, scale=1.0)), reads=[flb], writes=[flb])
                S.add("dve", (lambda e, fl=fl: e.tensor_scalar(out=fl[:], in0=fl[:], scalar1=-1.0, scalar2=None, op0=ALU.mult)),
                      reads=[flb], writes=[flb])
                fl_tiles.append((fl, flb))
        def cumsum_block():
            for t in range(4):
                fl, flb = fl_tiles[t]
                o0 = 32 + 16 * t
                S.add("pe", (lambda e, fl=fl, o0=o0: (e.matmul(pm[:, o0:o0 + 8], lhsT=tri_f, rhs=fl[:], start=True, stop=True),
                                                        e.matmul(pm[:, o0 + 8:o0 + 16], lhsT=ones_f, rhs=fl[:], start=True, stop=True))[1]),
                      reads=[flb, B_cf32], writes=[B_pm])
            for t in range(4):
                T = 4 * I + t
                o0 = 32 + 16 * t
                S.add("dve", (lambda e, T=T, o0=o0: e.tensor_tensor(out=c_all[:, T, :], in0=pm[:, o0:o0 + 8], in1=carry[:, T, :], op=ALU.add)),
                      reads=[B_pm, B_carry[T]], writes=[B_c[T]])
                S.add("dve", (lambda e, T=T, o0=o0: e.tensor_tensor(out=carry[:, T + 1, :], in0=pm[:, o0 + 8:o0 + 16], in1=carry[:, T, :], op=ALU.add)),
                      reads=[B_pm, B_carry[T]], writes=[B_carry[T + 1]])
        ring0 = (I % 2) * 512
        wcur = {}

        def proj_unit(name, c0, u, fill, k0=None, I=I, hT=hT, B_hT=B_hT, tok0=tok0, ring0=ring0, wcur=wcur):
            if u == 0 and (k0 is None or k0 == 0):
                wcur[name] = load_w(w_in_v[:, :, c0:c0 + 512])
            wt, wb = wcur[name]
            bk, bb = (bank[2] if fill else rotAll.next())
            eng = "dve" if fill else None
            if name in ("fq", "fk", "cq", "ck"):
                cc = u
                pairs_ = [(wt[:, k, cc * 128:(cc + 1) * 128], hT[:, k, :]) for k in range(8)]
                reads_ = [wb] + B_hT
            else:
                pairs_ = [(hT[:, k, u * 128:(u + 1) * 128], wt[:, k, :]) for k in range(8)]
                reads_ = [wb, B_hT[u]]
            if k0 is not None:
                l_, r_ = pairs_[k0]
                S.add("pe", (lambda e: e.matmul(bk[:, :], lhsT=l_, rhs=r_, start=(k0 == 0), stop=(k0 == 7))),
                      reads=reads_, writes=[bb])
                if k0 < 7:
                    return
            elif True:
                mm_group(bk[:, :], pairs_, reads=reads_, writes=[bb])
            if name in ("fq", "fk", "cq", "ck"):
                if name == "fq":
                    evac_q(fq_sb, B_fq, cc, bk, bb, eng=eng)
                elif name == "cq":
                    evac_q(cq_sb, B_cq, cc, bk, bb, eng=eng)
                elif name == "fk":
                    evac(fkT[:, cc, tok0:tok0 + 512], bk[:, :], [bb], B_fkT[4 * I:4 * I + 4], eng=eng)
                else:
                    evac(ckT[:, cc, ring0:ring0 + 512], bk[:, :], [bb], [B_ckT[(4 * I + t) % 8] for t in range(4)], eng=eng)
            else:
                t = u
                T = 4 * I + t
                src = bk[:, :].rearrange("p (m e d) -> p m e d", m=4, e=2, d=64)
                if name == "fv":
                    evac(fv[:, T, :, 0:3:2, :], src, [bb], [B_fv[T]], eng=eng)
                else:
                    evac(cv[:, T % 8, :, 0:3:2, :], src, [bb], [B_cv[T % 8]], eng=eng)

        for name, c0 in [("fq", 0), ("fk", 512), ("fv", 1024)]:
            for u in range(4):
                proj_unit(name, c0, u, False)
            if name == "fq":
                flogit_chain()
            if name == "fv":
                cumsum_block()
        fillers = [(lambda name=name, c0=c0, u=u, k0=k0: proj_unit(name, c0, u, True, k0=k0))
                   for name, c0 in [("cq", 1544), ("ck", 2056), ("cv", 2568)] for u in range(4) for k0 in range(8)]
        nj = 4 * I + 4
        bI, bIb = biasI[I % 2], B_biasI[I % 2]
        S.add("dve", (lambda e, bI=bI, nj=nj: e.tensor_tensor(
            out=bI[:, 0:nj, :], in0=carry[:, nj - 2, :].unsqueeze(1).to_broadcast([128, nj, 8]),
            in1=c_all[:, 0:nj, :], op=ALU.subtract)),
            reads=[B_carry[nj - 2]] + B_c[0:nj], writes=[bIb])

        if I == 0:
            mem_kv_prep()
        if stage == 2:
            while fillers:
                fillers.pop(0)()
            for cc in range(4):
                dump(fq_sb[:, 2 * cc, :], [B_fq[2 * cc]], cc * 128, 512)
                dump(fkT[:, cc, 0:512], B_fkT[0:4], 512 + cc * 128, 512)
            dump(fv[:, 0].rearrange("p a b c -> p (a b c)"), [B_fv[0]], 1024, 768)
            dump(c_all[:].rearrange("p a b -> p (a b)"), B_c[0:4], 1152, 128)
            dump(bI[:].rearrange("p a b -> p (a b)"), [bIb], 1280, 128)
            dump(cv[:, 1].rearrange("p a b c -> p (a b c)"), [B_cv[1]], 1408, 768)
            for cc in range(4):
                dump(cq_sb[:, 2 * cc, :], [B_cq[2 * cc]], 1536 + cc * 64, 512)
            return finish()
        yT, B_yT = actT.next()
        B_yTf = [Buf(f"yTf{t}") for t in range(4)]
        gf_bank = [None]
        B_yTck = [Buf(f"yTck{m}") for m in range(4)]

        def attn_group(kind):
            for m in range(4):
                Ob = [Opool.next(), Opool.next()]
                items = []
                if kind == "fox":
                    for j in range(nj):
                        for e_ in range(2):
                            items.append((e_, j))
                else:
                    jl = list(range(max(0, 4 * I - 4), 4 * I + 4))
                    jstar = max(4 * I - 1, 0)
                    jl.remove(jstar)
                    jl = [jstar] + jl
                    for j in jl:
                        for e_ in range(2):
                            items.append((e_, j))
                nitems = len(items)
                pend = []

                def stage_a(idx):
                    e_, j = items[idx]
                    h = 2 * m + e_
                    pr = slice(64 * e_, 64 * e_ + 64)
                    Sb, Sbb = Spool.next()
                    Pt, Ptb = ptile.next()
                    if kind == "fox":
                        r_ = j - 4 * I
                        n0, n1 = max(r_, 0) * 128, 512
                        kap = fkT[:, m, j * 128:(j + 1) * 128]
                        qap = fq_sb[:, h, n0:n1]
                        mm_group(Sb[:, n0:n1], [(kap, qap)], reads=[B_fkT[j], B_fq[h]], writes=[Sbb])
                        S.add("act", (lambda e, Pt=Pt, Sb=Sb, n0=n0, n1=n1, j=j, h=h, bI=bI: e.activation(
                            out=Pt[:, n0:n1], in_=Sb[:, n0:n1], func=AF.Exp, bias=bI[:, j, h:h + 1], scale=1.0)),
                            reads=[Sbb, bIb], writes=[Ptb])
                        if r_ >= 0:
                            S.add("dve", (lambda e, Pt=Pt, n0=n0: e.tensor_tensor(
                                out=Pt[:, n0:n0 + 128], in0=Pt[:, n0:n0 + 128], in1=tri_bf, op=ALU.mult)),
                                reads=[Ptb, B_cbf], writes=[Ptb])
                        vap = fv[:, j, m, e_:e_ + 2, :].rearrange("p a d -> p (a d)")
                        vbuf = B_fv[j]
                    else:
                        a_lo, a_hi = max(j, 4 * I), min(j + 4, 4 * I + 3)
                        n0, n1 = (a_lo - 4 * I) * 128, (a_hi - 4 * I + 1) * 128
                        js = j % 8
                        kap = ckT[:, m, js * 128:(js + 1) * 128]
                        qap = cq_sb[:, h, n0:n1]
                        mm_group(Sb[:, n0:n1], [(kap, qap)], reads=[B_ckT[js], B_cq[h]], writes=[Sbb])
                        S.add("act", (lambda e, Pt=Pt, Sb=Sb, n0=n0, n1=n1, h=h: e.activation(
                            out=Pt[:, n0:n1], in_=Sb[:, n0:n1], func=AF.Exp, bias=cch[:, h:h + 1], scale=1.0)),
                            reads=[Sbb, B_cch], writes=[Ptb])
                        aa = [a for a in (j, j + 1) if a_lo <= a <= a_hi]
                        if aa:
                            c_ = (aa[0] - 4 * I) * 128
                            eo = (aa[0] - j) * 128
                            w_ = 128 * len(aa)
                            S.add("dve", (lambda e, Pt=Pt, c_=c_, eo=eo, h=h, w_=w_: e.tensor_tensor(
                                out=Pt[:, c_:c_ + w_], in0=Pt[:, c_:c_ + w_], in1=E01[:, h, eo:eo + w_], op=ALU.mult)),
                                reads=[Ptb, B_E01], writes=[Ptb])
                        a = j + 4
                        if a_lo <= a <= a_hi:
                            c_ = (a - 4 * I) * 128
                            S.add("dve", (lambda e, Pt=Pt, c_=c_: e.tensor_tensor(
                                out=Pt[:, c_:c_ + 128], in0=Pt[:, c_:c_ + 128], in1=m4_bf, op=ALU.mult)),
                                reads=[Ptb, B_cbf], writes=[Ptb])
                        vap = cv[:, js, m, e_:e_ + 2, :].rearrange("p a d -> p (a d)")
                        vbuf = B_cv[js]
                    pend.append((idx, e_, Pt, Ptb, n0, n1, vap, vbuf))

                first_seen = [True, True]
                last_idx = [max(i for i, it in enumerate(items) if it[0] == e_) for e_ in range(2)]

                def stage_b():
                    idx, e_, Pt, Ptb, n0, n1, vap, vbuf = pend.pop(0)
                    Ot, Otb = Ob[e_]
                    fst = first_seen[e_]
                    first_seen[e_] = False
                    mm_group(Ot[:, n0:n1], [(vap, Pt[:, n0:n1])], reads=[Ptb, vbuf], writes=[Otb],
                             first=fst, last=(idx == last_idx[e_]))

                LOOK = 3
                DEFER = 8
                G = gcount[0]
                gcount[0] += 1
                for idx in range(nitems):
                    stage_a(idx)
                    if idx >= LOOK:
                        stage_b()
                    if kind == "fox" and fillers:
                        fillers.pop(0)()
                    if idx == min(DEFER, nitems - 1):
                        while deferred and deferred[0][0] <= G:
                            deferred.pop(0)[1]()
                while pend:
                    stage_b()

                def norm_pair(m=m, Ob=Ob):
                    rd, rdb = f32p.next()
                    for e_ in range(2):
                        Ot, Otb = Ob[e_]
                        num = slice(64 * e_, 64 * e_ + 64)
                        den = slice(64 * (1 - e_), 64 * (1 - e_) + 64)
                        S.add("act", (lambda e, rd=rd, Ot=Ot, num=num, den=den: e.activation(out=rd[num, :], in_=Ot[den, :], func=AF.Ln)),
                              reads=[Otb], writes=[rdb])
                    S.add("act", (lambda e, rd=rd: e.activation(out=rd[:], in_=rd[:], func=AF.Exp, scale=-1.0)),
                          reads=[rdb], writes=[rdb])
                    for e_ in range(2):
                        Ot, Otb = Ob[e_]
                        num = slice(64 * e_, 64 * e_ + 64)
                        S.add("dve", (lambda e, rd=rd, Ot=Ot, num=num, m=m: e.tensor_tensor(
                            out=y32[num, m, :], in0=Ot[num, :], in1=rd[num, :], op=ALU.mult)),
                            reads=[Otb, rdb], writes=[B_y32[m]])

                def sq_pair(m=m):
                    S.add("dve", (lambda e, m=m: e.tensor_tensor(out=sq_sb[:, m, :], in0=y32[:, m, :], in1=y32[:, m, :], op=ALU.mult)),
                          reads=[B_y32[m]], writes=[B_sq[m]])
                deferred.append((G + 1, norm_pair))
                deferred.append((G + 2, sq_pair))

            def group_final(kind=kind):
                ssb, ssbb = (gf_bank[0] if (kind == "chk" and gf_bank[0] is not None) else Spool.next())
                mm_group(ssb[:, :], [(ones_bf, sq_sb[:, m, :]) for m in range(4)], reads=B_sq + [B_cbf], writes=[ssbb])
                lnv, lnb = f32p.next()
                S.add("act", (lambda e: e.activation(out=lnv[:], in_=ssb[:, :], func=AF.Ln, bias=EPS, scale=1.0 / 512)),
                      reads=[ssbb], writes=[lnb])
                S.add("act", (lambda e: e.activation(out=lnv[:], in_=lnv[:], func=AF.Exp, scale=-0.5)), reads=[lnb], writes=[lnb])
                g0 = GC[kind]
                off = 0 if kind == "fox" else 4
                for m in range(4):
                    S.add("dve", (lambda e, m=m, yT=yT, off=off, g0=g0, lnv=lnv: e.scalar_tensor_tensor(
                        out=yT[:, off + m, :], in0=y32[:, m, :], scalar=gcol[:, g0 + m:g0 + m + 1], in1=lnv[:],
                        op0=ALU.mult, op1=ALU.mult)),
                        reads=[B_y32[m], lnb, B_gc[kind]], writes=(B_yTf + B_yT if kind == "fox" else B_yT + [B_yTck[m]]))
            deferred.append((gcount[0] + 1, group_final))

        deferred = []
        gcount = [0]
        attn_group("fox")
        while fillers:
            fillers.pop(0)()
        if stage == 3:
            while deferred:
                deferred.pop(0)[1]()
            for cc in range(4):
                dump(yT[:, cc, :], B_yT + B_yTf, cc * 128, 512)
            return finish()
        attn_group("chk")
        wo = [load_w(wview(w_out_d)[:, :, h * 512:(h + 1) * 512]) for h in range(2)]
        early = {}
        for t, bp in enumerate([(bank[0], bank[1]), (bank[2], bank[7])]):
            early[t] = []
            for h in range(2):
                bk, bb = bp[h]
                mm_group(bk[:, :], [(yT[:, k, t * 128:(t + 1) * 128], wo[h][0][:, k, :]) for k in range(4)],
                         reads=B_yTf + [wo[h][1]], writes=[bb], first=True, last=False)
                early[t].append((bk[:, :], bb))
        gf_bank[0] = bank[3]
        while deferred:
            deferred.pop(0)[1]()
        if stage == 4:
            for cc in range(8):
                dump(yT[:, cc, :], B_yT + B_yTf, cc * 128, 512)
            return finish()

        h2T, B_h2T = actT.next()
        out_proj_phase(yT, B_yT + B_yTf, wo, "mix_post", "mem_pre", h2T, B_h2T, early=early, B_late=B_yTck,
                       late_banks={2: (bank[3], bank[4]), 3: (bank[5], bank[6])})

        if stage == 5:
            for t in range(4):
                dump(xres[:, t, :], [B_xres[t]], t * 128, 1024)
            for cc in range(8):
                dump(h2T[:, cc, :], B_h2T, 512 + cc * 128, 512)
            return finish()
        wq = [load_w(wview(w_mq_d)[:, :, h * 512:(h + 1) * 512]) for h in range(2)]
        qT, B_qT = actT.next()
        for dc in range(8):
            wt, wb = wq[dc // 4]
            bk, bb = rotAll.next()
            mm_group(bk[:, :], [(wt[:, k, (dc % 4) * 128:(dc % 4) * 128 + 128], h2T[:, k, :]) for k in range(8)],
                     reads=[wb] + B_h2T, writes=[bb])
            evac(qT[:, dc, :], bk[:, :], [bb], B_qT, scale=1.0 / 16)
        oT, B_oT = actT.next()
        def mem_S(hd):
            Ps = []
            for mt in range(2):
                Sb, Sbb = Spool.next()
                mm_group(Sb[:, :], [(kmT[:, 2 * hd + c, mt * 128:(mt + 1) * 128], qT[:, 2 * hd + c, :]) for c in range(2)],
                         reads=[B_kmT] + B_qT, writes=[Sbb])
                Pt, Ptb = ptile.next()
                S.add("act", (lambda e, Pt=Pt, Sb=Sb: e.activation(out=Pt[:], in_=Sb[:, :], func=AF.Exp)),
                      reads=[Sbb], writes=[Ptb])
                Ps.append((Pt, Ptb))
            return Ps

        def mem_PV(hd, Ps):
            dn, dnb = Opool.next()
            mm_group(dn[:, :], [(ones_bf, Ps[mt][0][:]) for mt in range(2)], reads=[Ps[0][1], Ps[1][1], B_cbf], writes=[dnb])
            rd, rdb = f32p.next()
            S.add("act", (lambda e, rd=rd, dn=dn: e.activation(out=rd[:], in_=dn[:, :], func=AF.Ln)), reads=[dnb], writes=[rdb])
            S.add("act", (lambda e, rd=rd: e.activation(out=rd[:], in_=rd[:], func=AF.Exp, scale=-1.0)), reads=[rdb], writes=[rdb])
            for c in range(2):
                Oc, Ocb = Opool.next()
                col = (2 * hd + c) * 128
                mm_group(Oc[:, :], [(vm[:, mt, col:col + 128], Ps[mt][0][:]) for mt in range(2)],
                         reads=[Ps[0][1], Ps[1][1], B_vm], writes=[Ocb])
                S.add("dve", (lambda e, Oc=Oc, rd=rd, hd=hd, c=c, oT=oT: e.tensor_tensor(
                    out=oT[:, 2 * hd + c, :], in0=Oc[:, :], in1=rd[:], op=ALU.mult)),
                    reads=[Ocb, rdb], writes=B_oT)

        prevP = None
        for hd in range(4):
            Ps = mem_S(hd)
            if prevP is not None:
                mem_PV(hd - 1, prevP)
            prevP = Ps
        mem_PV(3, prevP)
        wmo = [load_w(wview(w_mo_d)[:, :, h * 512:(h + 1) * 512]) for h in range(2)]
        h3T, B_h3T = actT.next()
        out_proj_phase(oT, B_oT, wmo, "mem_post", "ff_pre", h3T, B_h3T)

        if stage == 6:
            for t in range(4):
                dump(xres[:, t, :], [B_xres[t]], t * 128, 1024)
            return finish()
        U = Rot([bank[0], bank[1], bank[2]])
        for g in range(8):
            wt, wb = load_w(wview(w_ff1_d)[:, :, g * 512:(g + 1) * 512])
            for cc in range(4):
                ffc = 4 * g + cc
                bk, bb = U.next()
                mm_group(bk[:, :], [(wt[:, k, cc * 128:(cc + 1) * 128], h3T[:, k, :]) for k in range(8)],
                         reads=[wb] + B_h3T, writes=[bb])
                rl, rlb = f32p.next()
                S.add("act", (lambda e, rl=rl, bk=bk: e.activation(out=rl[:], in_=bk[:, :], func=AF.Relu)), reads=[bb], writes=[rlb])
                S.add("dve", (lambda e, rl=rl, ffc=ffc: e.tensor_tensor(out=arena[:, ffc, :], in0=rl[:], in1=rl[:], op=ALU.mult)),
                      reads=[rlb], writes=[B_ar[ffc]])
            if I + 1 < NSB and stage > 7 and g in (1, 2, 3, 4, 5):
                if g == 1:
                    nh = actT.next()
                    npre = {}

                def _stage(t, I=I, npre=npre):
                    T = 4 * (I + 1) + t
                    xs_ap, xs_b = xstage[t % 2]
                    S.add("sp", (lambda e, xs_ap=xs_ap, T=T: e.dma_start(out=xs_ap, in_=x_d[T * 128:(T + 1) * 128, :])),
                          writes=xs_b, dma=True)
                    npre[t] = prenorm_pre(xs_ap, xs_b)

                def _tr(t, nh=nh, npre=npre):
                    prenorm_tr(npre[t][0], npre[t][1], "mix_pre", nh[0], nh[1][t], t * 128)
                if g == 1:
                    _stage(0)
                elif g == 2:
                    _stage(1)
                elif g == 3:
                    _tr(0)
                    _tr(1)
                    _stage(2)
                elif g == 4:
                    _stage(3)
                else:
                    _tr(2)
                    _tr(3)
                    next_hT = nh
        if stage == 70:
            for cc in range(4):
                dump(arena[:, cc, :], [B_ar[cc]], cc * 128, 512)
            return finish()
        acc = [[bank[6], bank[7]], [bank[0], bank[1]], [bank[2], bank[3]], [bank[4], bank[5]]]
        w2v = wview(w_ff2_d)
        def ffn2_post(t, I=I, acc=acc):
            T = 4 * I + t
            postnorm_residual((acc[t][0][0][:, :], acc[t][0][1]), (acc[t][1][0][:, :], acc[t][1][1]), "ff_post", t)
            S.add("sp", (lambda e, t=t, T=T: e.dma_start(out=out_d[T * 128:(T + 1) * 128, :], in_=xres[:, t, :])),
                  reads=[B_xres[t]], dma=True)
            if I + 1 < NSB and stage > 7:
                load_xres(I + 1, t)

        for kg in range(4):
            wts = [load_w(w2v[:, kg * 8:(kg + 1) * 8, half * 512:(half + 1) * 512]) for half in range(2)]
            order = ([(half, t) for half in range(2) for t in range(4)] if kg < 3
                     else [(0, 0), (1, 0)] + [(half, t) for half in range(2) for t in range(1, 4)])
            for half, t in order:
                wt, wb = wts[half]
                mm_group(acc[t][half][0][:, :],
                         [(arena[:, kg * 8 + i, t * 128:(t + 1) * 128], wt[:, i, :]) for i in range(8)],
                         reads=[wb] + B_ar[kg * 8:(kg + 1) * 8], writes=[acc[t][half][1]],
                         first=(kg == 0), last=(kg == 3))
                if kg == 3 and (half, t) == (1, 0):
                    ffn2_post(0)
        for t in range(1, 4):
            ffn2_post(t)
        if stage == 7:
            return finish()
    return finish()


_NC_CACHE = {}


def kernel(**inputs):
    f32 = lambda a: np.ascontiguousarray(np.asarray(a, dtype=np.float32))
    x = f32(inputs["x"])
    mem = f32(inputs["mem"])
    B = x.shape[0]
    shared = {}
    for k in ["w_in", "w_out", "w_mq", "w_mk", "w_mv", "w_mo", "w_ff1", "w_ff2"]:
        shared[k] = f32(inputs[k])[0]
    for k in ["b_fgt", "g_fox_out", "g_chk_out", "g_mix_pre", "g_mix_post", "g_mem_kv",
              "g_mem_pre", "g_mem_post", "g_ff_pre", "g_ff_post"]:
        shared[k] = f32(inputs[k]).reshape(-1)
    shared["rel_bias"] = f32(inputs["rel_bias"]).reshape(-1)
    shared["consts"] = host_consts()
    if "nc" not in _NC_CACHE:
        _NC_CACHE["nc"] = build()
    nc = _NC_CACHE["nc"]
    in_maps = []
    for b in range(B):
        m = dict(shared)
        m["x"] = x[b]
        m["mem"] = mem[b]
        in_maps.append(m)
    res = run_bass_kernel_spmd(nc, in_maps, core_ids=list(range(B)))
    return np.stack([r["out"] for r in res.results], axis=0).astype(np.float32)
```
